# Optimizing a Trainium2 kernel written in Bass

```python
import math
import jax, jax.numpy as jnp
from jax import lax
import numpy as np


D_MODEL = 1024
BATCH = 8
SEQ = 4096
DEPTH = 2

MEM_LEN = 256
CHUNK = 128
SSD_INNER = D_MODEL
SSD_HEAD_DIM = 64
SSD_HEADS = SSD_INNER // SSD_HEAD_DIM
SSD_GROUPS = 4
SSD_HPG = SSD_HEADS // SSD_GROUPS
SSD_STATE = 128
CONV_K = 4
SSD_CONV_DIM = SSD_INNER + 2 * SSD_GROUPS * SSD_STATE
GM_WIDTH = D_MODEL
GM_GROUPS = 8
GM_GROUP_DIM = GM_WIDTH // GM_GROUPS
XA_HEADS = 4
XA_HEAD_DIM = D_MODEL // XA_HEADS
N_EXPERTS = 32
TOP_K = 4
D_EXPERT = D_MODEL
SWIGLU_LIMIT = 7.0
SWIGLU_ALPHA = 1.702
DN_ALPHA = (2 * DEPTH) ** 0.25
DN_BETA = (8 * DEPTH) ** -0.25
EPS = 1e-5
OFF_Z = 0
OFF_XBC = OFF_Z + SSD_INNER
OFF_DT = OFF_XBC + SSD_CONV_DIM
OFF_U = OFF_DT + SSD_HEADS
OFF_V = OFF_U + GM_WIDTH
OFF_G = OFF_V + GM_WIDTH
IN_COLS = OFF_G + 2 * D_MODEL

kernel_name = 'hybrid_ssd_gmlp_xattn_moe_deepnorm'


def layer_norm(x, g, b):
    xf = x.astype(jnp.float32)
    mu = jnp.mean(xf, axis=-1, keepdims=True)
    var = jnp.mean(jnp.square(xf - mu), axis=-1, keepdims=True)
    return ((xf - mu) * lax.rsqrt(var + EPS) * g + b).astype(x.dtype)


def causal_dwconv(x, w, b):
    s = x.shape[1]
    xp = jnp.pad(x, ((0, 0), (CONV_K - 1, 0), (0, 0)))
    out = xp[:, 0:s] * w[0]
    for k in range(1, CONV_K):
        out = out + xp[:, k:k + s] * w[k]
    return out + b


def ssd_chunked(xh, dt, a, bm, cm):
    b, s = xh.shape[0], xh.shape[1]
    c = s // CHUNK
    f32 = jnp.float32
    X = (xh.astype(f32) * dt[..., None]).reshape(b, c, CHUNK, SSD_GROUPS, SSD_HPG, SSD_HEAD_DIM)
    dA = (dt * a).reshape(b, c, CHUNK, SSD_GROUPS, SSD_HPG).transpose(0, 1, 3, 4, 2)
    a_cs = jnp.cumsum(dA, axis=-1)
    Bc = bm.astype(f32).reshape(b, c, CHUNK, SSD_GROUPS, SSD_STATE)
    Cc = cm.astype(f32).reshape(b, c, CHUNK, SSD_GROUPS, SSD_STATE)
    mask = jnp.tril(jnp.ones((CHUNK, CHUNK), dtype=bool))
    lmat = jnp.exp(jnp.where(mask, a_cs[..., :, None] - a_cs[..., None, :], -jnp.inf))
    cb = jnp.einsum('bclgn,bcsgn->bcgls', Cc, Bc)
    y_diag = jnp.einsum('bcgls,bcgrls,bcsgrp->bclgrp', cb, lmat, X)
    decay_states = jnp.exp(a_cs[..., -1:] - a_cs)
    states = jnp.einsum('bclgn,bcgrl,bclgrp->bcgrpn', Bc, decay_states, X)
    chunk_decay = jnp.exp(a_cs[..., -1])

    def step(h, inp):
        st, d = inp
        return d[..., None, None] * h + st, h

    h0 = jnp.zeros((b, SSD_GROUPS, SSD_HPG, SSD_HEAD_DIM, SSD_STATE), f32)
    _, prev = lax.scan(step, h0, (states.transpose(1, 0, 2, 3, 4, 5), chunk_decay.transpose(1, 0, 2, 3)))
    prev = prev.transpose(1, 0, 2, 3, 4, 5)
    y_off = jnp.einsum('bclgn,bcgrpn,bcgrl->bclgrp', Cc, prev, jnp.exp(a_cs))
    return (y_diag + y_off).reshape(b, s, SSD_HEADS, SSD_HEAD_DIM)


def hybrid_mixer(x, w_in, conv_w, conv_b, dt_bias, a_log, d_skip, ssd_norm_w,
                 gm_ln_g, gm_ln_b, w_sp, b_sp, p_ssd, p_gm, w_out):
    b, s, _ = x.shape
    f32 = jnp.float32
    zx = x @ w_in
    z = zx[..., OFF_Z:OFF_XBC]
    xbc = zx[..., OFF_XBC:OFF_DT]
    dt_raw = zx[..., OFF_DT:OFF_U]
    u = zx[..., OFF_U:OFF_V]
    v = zx[..., OFF_V:OFF_G]
    g_ssd = zx[..., OFF_G:OFF_G + D_MODEL]
    g_gm = zx[..., OFF_G + D_MODEL:]

    xbc = jax.nn.silu(causal_dwconv(xbc, conv_w, conv_b))
    gn = SSD_GROUPS * SSD_STATE
    xs = xbc[..., :SSD_INNER].reshape(b, s, SSD_HEADS, SSD_HEAD_DIM)
    bm = xbc[..., SSD_INNER:SSD_INNER + gn].reshape(b, s, SSD_GROUPS, SSD_STATE)
    cm = xbc[..., SSD_INNER + gn:].reshape(b, s, SSD_GROUPS, SSD_STATE)
    dt = jax.nn.softplus(dt_raw.astype(f32) + dt_bias.astype(f32))
    a = -jnp.exp(a_log.astype(f32))
    y = ssd_chunked(xs, dt, a, bm, cm) + d_skip.astype(f32)[:, None] * xs.astype(f32)
    y = y.reshape(b, s, SSD_INNER) * jax.nn.silu(z.astype(f32))
    yg = y.reshape(b, s, SSD_GROUPS, SSD_INNER // SSD_GROUPS)
    yg = yg * lax.rsqrt(jnp.mean(jnp.square(yg), axis=-1, keepdims=True) + EPS)
    y_ssd = (yg.reshape(b, s, SSD_INNER) * ssd_norm_w).astype(x.dtype)

    u = jax.nn.gelu(u, approximate=False)
    v = layer_norm(jax.nn.gelu(v, approximate=False), gm_ln_g, gm_ln_b)
    vc = v.reshape(b, s // CHUNK, CHUNK, GM_GROUPS, GM_GROUP_DIM)
    mask = jnp.tril(jnp.ones((CHUNK, CHUNK), dtype=bool))
    w_causal = jnp.where(mask, w_sp, 0.0).astype(v.dtype)
    sv = jnp.einsum('gts,bcsgd->bctgd', w_causal, vc) + b_sp.T[:, :, None]
    y_gm = u * sv.reshape(b, s, GM_WIDTH)

    h = jax.nn.sigmoid(g_ssd) * (y_ssd @ p_ssd) + jax.nn.sigmoid(g_gm) * (y_gm @ p_gm)
    return h @ w_out


def cross_attn(x, mem, wq, wk, wv, wo):
    b, s, _ = x.shape
    m = mem.shape[1]
    q = (x @ wq).reshape(b, s, XA_HEADS, XA_HEAD_DIM)
    k = (mem @ wk).reshape(b, m, XA_HEADS, XA_HEAD_DIM)
    v = (mem @ wv).reshape(b, m, XA_HEADS, XA_HEAD_DIM)
    sc = jnp.einsum('bshd,bmhd->bhsm', q, k).astype(jnp.float32) * (XA_HEAD_DIM ** -0.5)
    p = jax.nn.softmax(sc, axis=-1).astype(x.dtype)
    o = jnp.einsum('bhsm,bmhd->bshd', p, v).reshape(b, s, D_MODEL)
    return o @ wo


def moe(x, w_router, b_router, w_gu, b_gu, w_down, b_down):
    shp = x.shape
    f32 = jnp.float32
    xt = x.reshape(-1, shp[-1])
    logits = (xt @ w_router + b_router).astype(f32)
    top_v, top_i = lax.top_k(logits, TOP_K)
    top_w = jax.nn.softmax(top_v, axis=-1)
    gates = jnp.sum(jax.nn.one_hot(top_i, N_EXPERTS, dtype=f32) * top_w[..., None], axis=1)

    def expert_step(acc, p):
        wgu, bgu, wd, bd, g = p
        hgu = xt @ wgu + bgu
        gate = jnp.minimum(hgu[:, :D_EXPERT], SWIGLU_LIMIT)
        up = jnp.clip(hgu[:, D_EXPERT:], -SWIGLU_LIMIT, SWIGLU_LIMIT)
        glu = gate * jax.nn.sigmoid(SWIGLU_ALPHA * gate)
        y = ((up + 1.0) * glu) @ wd + bd
        return acc + g[:, None] * y.astype(f32), None

    acc0 = jnp.zeros((xt.shape[0], shp[-1]), f32)
    acc, _ = lax.scan(expert_step, acc0, (w_gu, b_gu, w_down, b_down, gates.T))
    return acc.astype(x.dtype).reshape(shp)


def setup_inputs(seed: int = 0) -> dict:
    key = jax.random.key(seed)
    ks = jax.random.split(key, 40)
    f32 = jnp.float32

    def nrm(k, shape, scale):
        return jax.random.normal(k, shape, f32) * scale

    dsc = D_MODEL ** -0.5
    u_dt = jax.random.uniform(ks[8], (DEPTH, SSD_HEADS), f32)
    dt0 = jnp.exp(u_dt * (math.log(0.1) - math.log(0.001)) + math.log(0.001))
    dt0 = jnp.maximum(dt0, 1e-4)
    dt_bias = dt0 + jnp.log(-jnp.expm1(-dt0))
    a_log = jnp.log(jax.random.uniform(ks[9], (DEPTH, SSD_HEADS), f32, 1.0, 16.0))
    return {
        'x': nrm(ks[0], (BATCH, SEQ, D_MODEL), 1.0),
        'mem': nrm(ks[1], (BATCH, MEM_LEN, D_MODEL), 1.0),
        'ln0_g': 1.0 + nrm(ks[2], (D_MODEL,), 0.02),
        'ln0_b': nrm(ks[3], (D_MODEL,), 0.02),
        'w_in': nrm(ks[4], (DEPTH, D_MODEL, IN_COLS), dsc),
        'conv_w': nrm(ks[5], (DEPTH, CONV_K, SSD_CONV_DIM), CONV_K ** -0.5),
        'conv_b': nrm(ks[6], (DEPTH, SSD_CONV_DIM), 0.02),
        'dt_bias': dt_bias,
        'a_log': a_log,
        'd_skip': 1.0 + nrm(ks[10], (DEPTH, SSD_HEADS), 0.1),
        'ssd_norm_w': 1.0 + nrm(ks[11], (DEPTH, SSD_INNER), 0.02),
        'gm_ln_g': 1.0 + nrm(ks[12], (DEPTH, GM_WIDTH), 0.02),
        'gm_ln_b': nrm(ks[13], (DEPTH, GM_WIDTH), 0.02),
        'w_sp': nrm(ks[14], (DEPTH, GM_GROUPS, CHUNK, CHUNK), CHUNK ** -0.5),
        'b_sp': 1.0 + nrm(ks[15], (DEPTH, GM_GROUPS, CHUNK), 0.02),
        'p_ssd': nrm(ks[16], (DEPTH, SSD_INNER, D_MODEL), SSD_INNER ** -0.5),
        'p_gm': nrm(ks[17], (DEPTH, GM_WIDTH, D_MODEL), GM_WIDTH ** -0.5),
        'w_out': nrm(ks[18], (DEPTH, D_MODEL, D_MODEL), dsc * DN_BETA),
        'wq': nrm(ks[19], (DEPTH, D_MODEL, D_MODEL), dsc),
        'wk': nrm(ks[20], (DEPTH, D_MODEL, D_MODEL), dsc),
        'wv': nrm(ks[21], (DEPTH, D_MODEL, D_MODEL), dsc * DN_BETA),
        'wo': nrm(ks[22], (DEPTH, D_MODEL, D_MODEL), dsc * DN_BETA),
        'w_router': nrm(ks[23], (DEPTH, D_MODEL, N_EXPERTS), dsc),
        'b_router': nrm(ks[24], (DEPTH, N_EXPERTS), 0.01),
        'w_gu': nrm(ks[25], (DEPTH, N_EXPERTS, D_MODEL, 2 * D_EXPERT), dsc * DN_BETA),
        'b_gu': nrm(ks[26], (DEPTH, N_EXPERTS, 2 * D_EXPERT), 0.01),
        'w_down': nrm(ks[27], (DEPTH, N_EXPERTS, D_EXPERT, D_MODEL), D_EXPERT ** -0.5 * DN_BETA),
        'b_down': nrm(ks[28], (DEPTH, N_EXPERTS, D_MODEL), 0.01),
        'ln_g': 1.0 + nrm(ks[29], (DEPTH, 3, D_MODEL), 0.02),
        'ln_b': nrm(ks[30], (DEPTH, 3, D_MODEL), 0.02),
    }


def reference(x, mem, ln0_g, ln0_b, w_in, conv_w, conv_b, dt_bias, a_log, d_skip,
              ssd_norm_w, gm_ln_g, gm_ln_b, w_sp, b_sp, p_ssd, p_gm, w_out,
              wq, wk, wv, wo, w_router, b_router, w_gu, b_gu, w_down, b_down,
              ln_g, ln_b):
    x = layer_norm(x, ln0_g, ln0_b)
    for l in range(DEPTH):
        mix = hybrid_mixer(x, w_in[l], conv_w[l], conv_b[l], dt_bias[l], a_log[l], d_skip[l],
                           ssd_norm_w[l], gm_ln_g[l], gm_ln_b[l], w_sp[l], b_sp[l],
                           p_ssd[l], p_gm[l], w_out[l])
        x = layer_norm(DN_ALPHA * x + mix, ln_g[l, 0], ln_b[l, 0])
        xa = cross_attn(x, mem, wq[l], wk[l], wv[l], wo[l])
        x = layer_norm(DN_ALPHA * x + xa, ln_g[l, 1], ln_b[l, 1])
        ff = moe(x, w_router[l], b_router[l], w_gu[l], b_gu[l], w_down[l], b_down[l])
        x = layer_norm(DN_ALPHA * x + ff, ln_g[l, 2], ln_b[l, 2])
    return x
```

```python
import numpy as np
import os as _os
from contextlib import ExitStack
import ml_dtypes
import concourse.bass as bass
import concourse.mybir as mybir
from concourse.bass_utils import run_bass_kernel_spmd

F32 = mybir.dt.float32
BF16 = mybir.dt.bfloat16
AF = mybir.ActivationFunctionType
ALU = mybir.AluOpType
AX = mybir.AxisListType

D = 1024
MEM = 256
NE = 32
ALPHA = float(4 ** 0.25)
EPS = 1e-5
OFF_Z, OFF_XBC, OFF_DT, OFF_U, OFF_V, OFF_G = 0, 1024, 3072, 3088, 4112, 5136


class View:
    __slots__ = ("buf", "ap")

    def __init__(self, buf, ap):
        self.buf = buf
        self.ap = ap

    def __getitem__(self, k):
        return View(self.buf, self.ap[k])


class Buf:
    def __init__(self, mk, t, name, is_sb=True):
        self.mk = mk
        self.t = t
        self.name = name
        self.w = None
        self.r = {}
        self.dsem = None
        self.is_sb = is_sb

    def __getitem__(self, k):
        return View(self, self.t[k])

    def v(self, ap):
        return View(self, ap)


class Eng:
    def __init__(self, mk, eng, name):
        self.mk = mk
        self.eng = eng
        self.name = name
        self.sem = mk.new_sem("e_" + name)
        self.cnt = 0
        self.seen = {}

    def wait(self, ev):
        if ev is None:
            return
        sem, val = ev[0], ev[1]
        if self.seen.get(sem, 0) >= val:
            return
        self.eng.wait_ge(sem, val)
        self.seen[sem] = val

    def deps(self, outs, ins):
        for v in ins:
            self.wait(v.buf.w)
            if getattr(v.buf, "is_psum", False):
                for sem, (val, en) in v.buf.r.items():
                    if en != self.name:
                        self.wait((sem, val))
        for v in outs:
            b = v.buf
            if b.w is not None and b.w[2] != self.name:
                self.wait(b.w)
            for sem, (val, en) in b.r.items():
                if en != self.name:
                    self.wait((sem, val))

    def done(self, instr, outs, ins):
        self.cnt += 1
        instr.then_inc(self.sem, 1)
        for v in ins:
            v.buf.r[self.sem] = (self.cnt, self.name)
        for v in outs:
            v.buf.w = (self.sem, self.cnt, self.name)
            v.buf.r = {}

    def op(self, fname, outs, ins, **kw):
        self.deps(outs, ins)
        k = {kk: (x.ap if isinstance(x, View) else x) for kk, x in kw.items()}
        instr = getattr(self.eng, fname)(**k)
        self.done(instr, outs, ins)
        return instr

    def dma(self, out, in_, **kw):
        b = out.buf if out.buf.is_sb else in_.buf
        if b.dsem is None:
            b.dsem = self.mk.take_dsem(self.name == "pool")
            b.dsem_sw = (self.name == "pool")
        assert b.dsem_sw == (self.name == "pool")
        self.deps([out], [in_])
        instr = self.eng.dma_start(out=out.ap, in_=in_.ap, **kw)
        b.dsem[1] += 16
        instr.then_inc(b.dsem[0], 16)
        in_.buf.r[b.dsem[0]] = (b.dsem[1], "dma")
        out.buf.w = (b.dsem[0], b.dsem[1], "dma")
        out.buf.r = {}
        self.mk.dma_events[b.dsem[0]] = b.dsem[1]
        return instr


class MK:
    def __init__(self, nc):
        self.nc = nc
        self.es = ExitStack()
        self.nsem = 0
        self.free_dsems = []
        self.free_dsems_sw = []
        self.dma_events = {}
        self.PE = Eng(self, nc.tensor, "pe")
        self.ACT = Eng(self, nc.scalar, "act")
        self.DVE = Eng(self, nc.vector, "dve")
        self.POOL = Eng(self, nc.gpsimd, "pool")
        self.SP = Eng(self, nc.sync, "sp")
        self.engs = [self.PE, self.ACT, self.DVE, self.POOL, self.SP]
        self.phase_stack = None
        self.phase_bufs = []
        self.uid = 0

    def new_sem(self, name):
        self.nsem += 1
        return self.es.enter_context(self.nc.semaphore(f"{name}_{self.nsem}"))

    def take_dsem(self, sw):
        fl = self.free_dsems_sw if sw else self.free_dsems
        if fl:
            return fl.pop()
        return [self.new_sem("dsw" if sw else "d"), 0]

    def _alloc(self, fn, name, shape, dt, persist):
        st = self.es if (persist or self.phase_stack is None) else self.phase_stack
        self.uid += 1
        t = st.enter_context(fn(f"{name}_{self.uid}", list(shape), dt))
        b = Buf(self, t, name, True)
        if st is not self.es:
            self.phase_bufs.append(b)
        return b

    def sb(self, name, shape, dt, persist=False):
        return self._alloc(self.nc.sbuf_tensor, name, shape, dt, persist)

    def ps(self, name, shape, dt, persist=False):
        b = self._alloc(self.nc.psum_tensor, name, shape, dt, persist)
        b.is_psum = True
        return b

    def dram(self, name, shape, dt, kind="Internal"):
        t = self.nc.dram_tensor(name, list(shape), dt, kind=kind).ap()
        return Buf(self, t, name, False)

    def barrier(self):
        for sem, val in self.dma_events.items():
            self.SP.wait((sem, val))
        self.dma_events = {}
        instr = self.SP.eng.nop()
        self.SP.cnt += 1
        instr.then_inc(self.SP.sem, 1)
        for e in self.engs:
            for o in self.engs:
                if o is not e and o.cnt > 0:
                    e.wait((o.sem, o.cnt))

    def begin_phase(self):
        self.phase_stack = ExitStack()
        self.phase_bufs = []

    def end_phase(self):
        self.barrier()
        for b in self.phase_bufs:
            if b.dsem is not None:
                (self.free_dsems_sw if b.dsem_sw else self.free_dsems).append(b.dsem)
                b.dsem = None
        self.phase_stack.close()
        self.phase_stack = None
        self.phase_bufs = []

    def finish(self):
        self.barrier()
        self.es.close()


def E(eng, fname, out=None, accum_out=None, **kw):
    outs = [out] + ([accum_out] if accum_out is not None else [])
    ins = [v for v in kw.values() if isinstance(v, View)]
    kk = dict(kw)
    kk["out"] = out
    if accum_out is not None:
        kk["accum_out"] = accum_out
    return eng.op(fname, outs, ins, **kk)


def tiles_of(mk, dbuf, nt):
    return [Buf(mk, dbuf.t[i * 128:(i + 1) * 128, :], f"{dbuf.name}_{i}", False) for i in range(nt)]


def build(S, depth, dbg=False, stop_after=None):
    NT = S // 128
    nc = bass.Bass("TRN2", target_bir_lowering=False)
    mk = MK(nc)
    PE, ACT, DVE, POOL, SP = mk.PE, mk.ACT, mk.DVE, mk.POOL, mk.SP
    L = depth

    def din(name, shape, dt=F32):
        return mk.dram(name, shape, dt, kind="ExternalInput")

    x_in = din("x", [S, D])
    mem_in = din("mem", [MEM, D])
    ln0 = din("ln0", [2, D])
    w_in = din("w_in", [L, D, 7184])
    conv_w = din("conv_w", [L, 128, 16, 4])
    conv_b = din("conv_b", [L, 128, 16])
    vec16 = din("vec16", [L, 3, 16])
    ssd_norm_w = din("ssd_norm_w", [L, D])
    gm_ln = din("gm_ln", [L, 2, D])
    w_spT = din("w_spT", [L, 128, 8, 128])
    b_spT = din("b_spT", [L, 128, 8])
    p_ssd = din("p_ssd", [L, D, D])
    p_gm = din("p_gm", [L, D, D])
    w_out = din("w_out", [L, D, D])
    wq = din("wq", [L, D, D])
    wk = din("wk", [L, D, D])
    wv = din("wv", [L, D, D])
    wo = din("wo", [L, D, D])
    w_router = din("w_router", [L, D, NE])
    b_router = din("b_router", [L, NE])
    w_gu = din("w_gu", [L, NE, D, 2 * D])
    b_guT = din("b_guT", [L, 128, NE, 16])
    w_down = din("w_down", [L, NE, D, D])
    b_down = din("b_down", [L, NE, D])
    ln_gb = din("ln_gb", [L, 3, 2, D])
    cf32 = din("cf32", [128, 4, 128])
    cidb = din("cidb", [128, 128], BF16)
    out_d = mk.dram("y_out", [S, D], F32, kind="ExternalOutput")

    skind = "ExternalOutput" if dbg else "Internal"
    X0 = mk.dram("X0", [S, D], F32, kind=skind)
    scr = []
    for l in range(L):
        d = {}
        for nm in ("H1", "X1", "X2"):
            d[nm] = mk.dram(f"{nm}_{l}", [S, D], F32, kind=skind)
        d["X3"] = out_d if l == L - 1 else mk.dram(f"X3_{l}", [S, D], F32, kind=skind)
        scr.append(d)

    cf = mk.sb("cf", [128, 4, 128], F32, persist=True)
    idb = mk.sb("idb", [128, 128], BF16, persist=True)
    memT = mk.sb("memT", [128, 8, MEM], BF16, persist=True)
    SP.dma(cf[:], cf32[:, :, :])
    SP.dma(idb[:], cidb[:, :])
    epsb = mk.sb("epsb", [128, 1], F32, persist=True)
    mk.epsb = epsb
    POOL.op("memset", [epsb[:]], [], ap=epsb[:], constant=EPS)
    identf = cf[:, 0, :]
    tri = cf[:, 1, :]
    ustr = cf[:, 2, :]
    ones = cf[:, 3, :]

    def MM(out, pairs):
        reads = [v for p in pairs for v in p]
        PE.deps([out], reads)
        n = len(pairs)
        ins = None
        for i, (l_, r_) in enumerate(pairs):
            ins = nc.tensor.matmul(out.ap, lhsT=l_.ap, rhs=r_.ap, start=(i == 0), stop=(i == n - 1))
        PE.done(ins, [out], reads)

    def TR(out, in_, ident):
        E(PE, "transpose", out=out, in_=in_, identity=ident)

    def bc_load(buf, src_row_ap):
        SP.dma(buf[:], Buf(mk, src_row_ap.partition_broadcast(128), "bc", False)[:])

    def dsrc(dbuf, ap):
        return View(dbuf, ap)

    def wload(eng, dst, dbuf, ap2d, c0, c1):
        v = ap2d.rearrange("(k p) n -> p k n", p=128)
        eng.dma(dst[:], dsrc(dbuf, v[:, :, c0:c1]))

    def layer_norm(t, dst, g_bc, b_bc, st, mv, rs):
        E(DVE, "bn_stats", out=st[:, 0:6], in_=t[:, 0:512])
        E(DVE, "bn_stats", out=st[:, 6:12], in_=t[:, 512:1024])
        E(DVE, "bn_aggr", out=mv[:], in_=st[:])
        E(ACT, "activation", out=rs[:], in_=mv[:, 1:2], func=AF.Sqrt, bias=epsb[:], scale=1.0)
        E(DVE, "reciprocal", out=rs[:], in_=rs[:])
        E(DVE, "tensor_scalar", out=t[:], in0=t[:], scalar1=mv[:, 0:1], scalar2=rs[:], op0=ALU.subtract, op1=ALU.mult)
        E(DVE, "tensor_tensor", out=t[:], in0=t[:], in1=g_bc, op=ALU.mult)
        E(DVE, "tensor_tensor", out=dst, in0=t[:], in1=b_bc, op=ALU.add)

    def make_xT(xt, xb, pTb, xT):
        E(ACT, "activation", out=xb[:], in_=xt[:], func=AF.Copy)
        for k in range(8):
            TR(pTb[:, k, :], xb[:, k * 128:(k + 1) * 128], idb[:])
        E(DVE, "tensor_copy", out=xT[:], in_=pTb[:])

    def proj1024(Wps, lhsT3, w3):
        for hf in range(2):
            MM(Wps[:, hf * 512:(hf + 1) * 512],
               [(lhsT3[:, k, :], w3[:, k, hf * 512:(hf + 1) * 512]) for k in range(8)])

    def run_all(gen):
        for _ in gen:
            pass

    def interleave(ga, gb):
        da = db = False
        while not (da and db):
            if not db:
                try:
                    next(gb)
                except StopIteration:
                    db = True
            if not da:
                try:
                    next(ga)
                except StopIteration:
                    da = True


    X0t = tiles_of(mk, X0, NT)
    xin_t = tiles_of(mk, x_in, NT)
    mk.begin_phase()
    g0 = mk.sb("g0", [128, D], F32)
    b0 = mk.sb("b0", [128, D], F32)
    bc_load(g0, ln0.t[0:1, :])
    bc_load(b0, ln0.t[1:2, :])
    xin = [mk.sb(f"xin{i}", [128, D], F32) for i in range(3)]
    st = mk.sb("st", [128, 12], F32)
    mv = mk.sb("mv", [128, 2], F32)
    rs = mk.sb("rs", [128, 1], F32)
    mt = mk.sb("mt", [128, 2, D], F32)
    mtb = mk.sb("mtb", [128, 2, D], BF16)
    pTb = mk.ps("pTb", [128, 8, 128], BF16)
    SP.dma(mt[:], dsrc(mem_in, mem_in.t.rearrange("(a p) d -> p a d", p=128)))
    E(ACT, "activation", out=mtb[:], in_=mt[:], func=AF.Copy)
    for a in range(2):
        for k in range(8):
            TR(pTb[:, k, :], mtb[:, a, k * 128:(k + 1) * 128], idb[:])
        E(DVE, "tensor_copy", out=memT[:, :, a * 128:(a + 1) * 128], in_=pTb[:])
    SP.dma(xin[0][:], xin_t[0][:, :])
    for i in range(NT):
        if i + 1 < NT:
            SP.dma(xin[(i + 1) % 3][:], xin_t[i + 1][:, :])
        xt = xin[i % 3]
        layer_norm(xt, xt[:], g0[:], b0[:], st, mv, rs)
        SP.dma(X0t[i][:, :], xt[:])
    mk.end_phase()
    if stop_after == "0":
        mk.finish()
        return nc

    Xcur = X0t
    for l in range(L):
        H1t = tiles_of(mk, scr[l]["H1"], NT)
        X1t = tiles_of(mk, scr[l]["X1"], NT)
        X2t = tiles_of(mk, scr[l]["X2"], NT)
        X3t = tiles_of(mk, scr[l]["X3"], NT)
        win2 = w_in.t[l]

        mk.begin_phase()
        wz = mk.sb("wz", [128, 8, 1024], BF16)
        wxbc = mk.sb("wxbc", [128, 8, 2048], BF16)
        wdt = mk.sb("wdt", [128, 8, 16], BF16)
        wg = mk.sb("wg", [128, 8, 1024], BF16)
        wps = mk.sb("wps", [128, 8, 1024], BF16)
        wload(POOL, wz, w_in, win2, OFF_Z, OFF_Z + 1024)
        wload(POOL, wxbc, w_in, win2, OFF_XBC, OFF_XBC + 2048)
        wload(POOL, wdt, w_in, win2, OFF_DT, OFF_DT + 16)
        wload(POOL, wg, w_in, win2, OFF_G, OFF_G + 1024)
        wload(POOL, wps, p_ssd, p_ssd.t[l], 0, 1024)
        cw = mk.sb("cw", [128, 16, 4], F32)
        cb = mk.sb("cb", [128, 16], F32)
        SP.dma(cw[:], conv_w[l, :, :, :])
        SP.dma(cb[:], conv_b[l, :, :])
        v16 = mk.sb("v16", [128, 3, 16], F32)
        for j in range(3):
            SP.dma(v16[:, j, :], Buf(mk, vec16.t[l, j:j + 1, :].partition_broadcast(128), "bc", False)[:])
        normw = mk.sb("normw", [128, D], F32)
        bc_load(normw, ssd_norm_w.t[l:l + 1, :])
        a_bc = mk.sb("a_bc", [128, 16], F32)
        E(ACT, "activation", out=a_bc[:], in_=v16[:, 1, :], func=AF.Exp)
        E(DVE, "tensor_scalar", out=a_bc[:], in0=a_bc[:], scalar1=-1.0, scalar2=None, op0=ALU.mult)
        dtb = v16[:, 0, :]
        dsk3 = v16.v(v16.t[:, 2, :].unsqueeze(2).to_broadcast([128, 16, 64]))

        xin = [mk.sb(f"xin{i}", [128, D], F32) for i in range(2)]
        xb = mk.sb("xb", [128, D], BF16)
        xT = mk.sb("xT", [128, 8, 128], BF16)
        sz2 = [mk.sb(f"sz{i}", [128, D], F32) for i in range(2)]
        sg2_ = [mk.sb(f"sg{i}", [128, D], F32) for i in range(2)]
        rawA = mk.sb("rawA", [128, 8, 131], F32)
        rawB = mk.sb("rawB", [128, 8, 131], F32)
        cvA = mk.sb("cvA", [128, 8, 128], F32)
        cvB = mk.sb("cvB", [128, 8, 128], F32)
        xsf2 = [mk.sb(f"xsf{i}", [128, 8, 128], F32) for i in range(2)]
        bcb2 = [mk.sb(f"bcb{i}", [128, 8, 128], BF16) for i in range(2)]
        ctA = mk.sb("ctA", [128, 8, 128], F32)
        ctB = mk.sb("ctB", [128, 8, 128], F32)
        xs_tm = mk.sb("xs_tm", [128, D], F32)
        Xb = mk.sb("Xb", [128, D], BF16)
        Xdb = mk.sb("Xdb", [128, D], BF16)
        Btm = mk.sb("Btm", [128, 4, 128], BF16)
        lhall = mk.sb("lhall", [128, 16, 128], F32)
        cbm = mk.sb("cbm", [128, 4, 128], F32)
        Eh = mk.sb("Eh", [128, 4, 128], F32)
        Mall = mk.sb("Mall", [128, 16, 128], BF16)
        yA = mk.sb("yA", [128, D], F32)
        yB = mk.sb("yB", [128, D], F32)
        y5b = mk.sb("y5b", [128, D], BF16)
        y5T = mk.sb("y5T", [128, 8, 128], BF16)
        h1t2 = [mk.sb(f"h1t{i}", [128, D], F32) for i in range(2)]
        prev = mk.sb("prev", [128, D], F32)
        prevb = mk.sb("prevb", [128, D], BF16)
        sm2_ = [mk.sb(f"sm{i}", [128, 12, 16], F32) for i in range(2)]
        ss = mk.sb("ss", [128, 8], F32)
        W0 = mk.ps("W0", [128, D], F32)
        W1 = mk.ps("W1", [128, D], F32)
        S0 = mk.ps("S0", [128, 512], F32)
        CB = mk.ps("CB", [128, 4, 128], F32)
        D0 = mk.ps("D0", [128, 4, 128], F32)
        pTb = mk.ps("pTb", [128, 8, 128], BF16)

        for b_ in (prev, prevb, rawA, rawB):
            POOL.op("memset", [b_[:]], [], ap=b_[:], constant=0.0)

        def v3(b, n=16, m=64):
            return b.v(b.t[:].rearrange("p (h d) -> p h d", h=n))

        def bc3(view2, n=16, m=64):
            return View(view2.buf, view2.ap.unsqueeze(2).to_broadcast([128, n, m]))

        cwA = [View(cw, cw.t[:, 0:8, k:k + 1].to_broadcast([128, 8, 128])) for k in range(4)]
        cwB = [View(cw, cw.t[:, 8:16, k:k + 1].to_broadcast([128, 8, 128])) for k in range(4)]
        cbA = View(cb, cb.t[:, 0:8].unsqueeze(2).to_broadcast([128, 8, 128]))
        cbB = View(cb, cb.t[:, 8:16].unsqueeze(2).to_broadcast([128, 8, 128]))

        def stage1(c):
            xt = xin[c % 2]
            sz, sg, xsf, bcb, sm = sz2[c % 2], sg2_[c % 2], xsf2[c % 2], bcb2[c % 2], sm2_[c % 2]
            make_xT(xt, xb, pTb, xT)
            yield
            for q in (2, 3, 0, 1):
                for j in range(4):
                    cc = q * 4 + j
                    MM(CB[:, j, :], [(wxbc[:, k, cc * 128:(cc + 1) * 128], xT[:, k, :]) for k in range(8)])
                dst = rawA if q < 2 else rawB
                E(ACT, "activation", out=dst[:, (q % 2) * 4:(q % 2) * 4 + 4, 3:131], in_=CB[:], func=AF.Copy)
                if q == 3:
                    conv_half(1)
                if q == 1:
                    conv_half(0)
                yield
            proj1024(W0, xT, wz)
            E(ACT, "activation", out=sz[:], in_=W0[:], func=AF.Silu)
            yield
            proj1024(W1, xT, wg)
            E(ACT, "activation", out=sg[:], in_=W1[:], func=AF.Sigmoid)
            yield
            MM(S0[:, 0:16], [(xT[:, k, :], wdt[:, k, :]) for k in range(8)])
            xx, ax, ee, dt, dA, acs, dd, ds, ea, cd, dtds = [sm[:, i, :] for i in range(11)]
            E(DVE, "tensor_tensor", out=xx, in0=S0[:, 0:16], in1=dtb, op=ALU.add)
            E(DVE, "tensor_scalar", out=ax, in0=xx, scalar1=-1.0, scalar2=None, op0=ALU.mult)
            E(DVE, "tensor_tensor", out=ax, in0=ax, in1=xx, op=ALU.min)
            E(ACT, "activation", out=ee, in_=ax, func=AF.Exp)
            E(ACT, "activation", out=ee, in_=ee, func=AF.Ln, bias=1.0)
            E(DVE, "scalar_tensor_tensor", out=dt, in0=xx, scalar=0.0, in1=ee, op0=ALU.max, op1=ALU.add)
            E(DVE, "tensor_tensor", out=dA, in0=dt, in1=a_bc[:], op=ALU.mult)
            yield
            MM(S0[:, 16:32], [(tri, dA)])
            MM(S0[:, 32:48], [(ones, dA)])
            E(ACT, "activation", out=acs, in_=S0[:, 16:32], func=AF.Copy)
            E(DVE, "tensor_tensor", out=dd, in0=S0[:, 32:48], in1=acs, op=ALU.subtract)
            E(ACT, "activation", out=ds, in_=dd, func=AF.Exp)
            E(ACT, "activation", out=ea, in_=acs, func=AF.Exp)
            E(ACT, "activation", out=cd, in_=S0[:, 32:48], func=AF.Exp)
            E(DVE, "tensor_tensor", out=dtds, in0=dt, in1=ds, op=ALU.mult)

        def conv_half(hb_):
            eng = POOL
            raw, cv, tmpc, cwk, cbk = ((rawA, cvA, ctA, cwA, cbA), (rawB, cvB, ctB, cwB, cbB))[hb_]
            E(eng, "tensor_tensor", out=cv[:], in0=raw[:, :, 0:128], in1=cwk[0], op=ALU.mult)
            E(eng, "tensor_tensor", out=cv[:], in0=cv[:], in1=cbk, op=ALU.add)
            for k in range(1, 4):
                E(eng, "tensor_tensor", out=tmpc[:], in0=raw[:, :, k:k + 128], in1=cwk[k], op=ALU.mult)
                E(eng, "tensor_tensor", out=cv[:], in0=cv[:], in1=tmpc[:], op=ALU.add)
            E(eng, "tensor_copy", out=raw[:, :, 0:3], in_=raw[:, :, 128:131])

        def stage1b_(c):
            pass

        def stage1c_(c):
            xsf, bcb = xsf2[c % 2], bcb2[c % 2]
            E(ACT, "activation", out=bcb[:], in_=cvB[:], func=AF.Silu)
            E(ACT, "activation", out=xsf[:], in_=cvA[:], func=AF.Silu)

        def stage2(c):
            sz, sg, xsf, bcb, sm = sz2[c % 2], sg2_[c % 2], xsf2[c % 2], bcb2[c % 2], sm2_[c % 2]
            xx, ax, ee, dt, dA, acs, dd, ds, ea, cd, dtds = [sm[:, i, :] for i in range(11)]
            for k in range(8):
                TR(W0[:, k * 128:(k + 1) * 128], xsf[:, k, :], identf)
            E(ACT, "activation", out=xs_tm[:], in_=W0[:], func=AF.Copy)
            E(DVE, "tensor_tensor", out=v3(Xb), in0=v3(xs_tm), in1=bc3(dt), op=ALU.mult)
            E(DVE, "tensor_tensor", out=v3(Xdb), in0=v3(xs_tm), in1=bc3(dtds), op=ALU.mult)
            for g in range(4):
                TR(pTb[:, g, :], bcb[:, g, :], idb[:])
            E(DVE, "tensor_copy", out=Btm[:], in_=pTb[:, 0:4, :])
            for g in range(4):
                MM(CB[:, g, :], [(bcb[:, g, :], bcb[:, 4 + g, :])])
            E(DVE, "tensor_tensor", out=cbm[:], in0=CB[:],
              in1=View(cf, tri.ap.unsqueeze(1).to_broadcast([128, 4, 128])), op=ALU.mult)
            E(DVE, "tensor_tensor", out=lhall[:],
              in0=View(cf, ustr.ap.unsqueeze(1).to_broadcast([128, 16, 128])),
              in1=View(dA.buf, dA.ap.unsqueeze(2).to_broadcast([128, 16, 128])), op=ALU.mult)
            yield
            for g in range(4):
                for r in range(4):
                    MM(D0[:, r, :], [(lhall[:, g * 4 + r, :], tri)])
                E(ACT, "activation", out=Eh[:], in_=D0[:], func=AF.Exp)
                E(DVE, "tensor_tensor", out=Mall[:, g * 4:(g + 1) * 4, :], in0=Eh[:],
                  in1=View(cbm, cbm.t[:, g:g + 1, :].to_broadcast([128, 4, 128])), op=ALU.mult)
                yield
            for g in range(4):
                MM(W1[:, g * 256:(g + 1) * 256], [(bcb[:, 4 + g, :], prevb[:, g * 256:(g + 1) * 256])])
            for h in range(16):
                MM(W0[:, h * 64:(h + 1) * 64], [(Mall[:, h, :], Xb[:, h * 64:(h + 1) * 64])])
            E(DVE, "tensor_tensor", out=v3(yA), in0=v3(W1), in1=bc3(ea), op=ALU.mult)
            E(DVE, "tensor_tensor", out=yA[:], in0=W0[:], in1=yA[:], op=ALU.add)
            E(DVE, "tensor_tensor", out=v3(yB), in0=v3(xs_tm), in1=dsk3, op=ALU.mult)
            E(DVE, "tensor_tensor", out=yA[:], in0=yA[:], in1=yB[:], op=ALU.add)
            E(DVE, "tensor_tensor", out=yA[:], in0=yA[:], in1=sz[:], op=ALU.mult)
            for g in range(4):
                E(ACT, "activation", out=yB[:, g * 256:(g + 1) * 256], in_=yA[:, g * 256:(g + 1) * 256],
                  func=AF.Square, accum_out=ss[:, g:g + 1])
            E(ACT, "activation", out=ss[:, 4:8], in_=ss[:, 0:4], func=AF.Sqrt, bias=epsb[:], scale=1.0 / 256)
            E(DVE, "reciprocal", out=ss[:, 4:8], in_=ss[:, 4:8])
            E(DVE, "tensor_tensor", out=v3(yA, 4), in0=v3(yA, 4),
              in1=View(ss, ss.t[:, 4:8].unsqueeze(2).to_broadcast([128, 4, 256])), op=ALU.mult)
            E(DVE, "tensor_tensor", out=y5b[:], in0=yA[:], in1=normw[:], op=ALU.mult)
            yield
            for g in range(4):
                MM(W1[:, g * 256:(g + 1) * 256], [(Btm[:, g, :], Xdb[:, g * 256:(g + 1) * 256])])
            E(DVE, "tensor_tensor", out=v3(prev), in0=v3(prev), in1=bc3(cd), op=ALU.mult)
            E(DVE, "tensor_tensor", out=prev[:], in0=W1[:], in1=prev[:], op=ALU.add)
            E(ACT, "activation", out=prevb[:], in_=prev[:], func=AF.Copy)
            yield
            for k in range(8):
                TR(pTb[:, k, :], y5b[:, k * 128:(k + 1) * 128], idb[:])
            E(DVE, "tensor_copy", out=y5T[:], in_=pTb[:])
            proj1024(W0, y5T, wps)
            h1t = h1t2[c % 2]
            E(DVE, "tensor_tensor", out=h1t[:], in0=W0[:], in1=sg[:], op=ALU.mult)
            SP.dma(H1t[c][:, :], h1t[:])

        SP.dma(xin[0][:], Xcur[0][:, :])
        if NT > 1:
            SP.dma(xin[1][:], Xcur[1][:, :])
        run_all(stage1(0))
        stage1b_(0)
        stage1c_(0)
        for c in range(NT):
            if c + 1 < NT:
                interleave(stage1(c + 1), stage2(c))
                stage1b_(c + 1)
                if c + 2 < NT:
                    SP.dma(xin[c % 2][:], Xcur[c + 2][:, :])
                stage1c_(c + 1)
            else:
                run_all(stage2(c))
        mk.end_phase()
        if stop_after == "A":
            mk.finish()
            return nc

        mk.begin_phase()
        wu = mk.sb("wu", [128, 8, 1024], BF16)
        wv_ = mk.sb("wv", [128, 8, 1024], BF16)
        wgg = mk.sb("wgg", [128, 8, 1024], BF16)
        wpg = mk.sb("wpg", [128, 8, 1024], BF16)
        wout = mk.sb("wout", [128, 8, 1024], BF16)
        wload(POOL, wu, w_in, win2, OFF_U, OFF_U + 1024)
        wload(POOL, wv_, w_in, win2, OFF_V, OFF_V + 1024)
        wload(POOL, wgg, w_in, win2, OFF_G + 1024, OFF_G + 2048)
        wload(POOL, wpg, p_gm, p_gm.t[l], 0, 1024)
        wload(POOL, wout, w_out, w_out.t[l], 0, 1024)
        wsp_f = mk.sb("wsp_f", [128, 8, 128], F32)
        wcT = mk.sb("wcT", [128, 8, 128], BF16)
        SP.dma(wsp_f[:], w_spT[l, :, :, :])
        E(DVE, "tensor_tensor", out=wcT[:], in0=wsp_f[:],
          in1=View(cf, tri.ap.unsqueeze(1).to_broadcast([128, 8, 128])), op=ALU.mult)
        bsp = mk.sb("bsp", [128, 8], F32)
        SP.dma(bsp[:], b_spT[l, :, :])
        gg = mk.sb("gg", [128, D], F32)
        gb = mk.sb("gb", [128, D], F32)
        lg = mk.sb("lg", [128, D], F32)
        lb = mk.sb("lb", [128, D], F32)
        bc_load(gg, gm_ln.t[l, 0:1, :])
        bc_load(gb, gm_ln.t[l, 1:2, :])
        bc_load(lg, ln_gb.t[l, 0, 0:1, :])
        bc_load(lb, ln_gb.t[l, 0, 1:2, :])
        xin = [mk.sb(f"xin{i}", [128, D], F32) for i in range(3)]
        h1in = [mk.sb(f"h1in{i}", [128, D], F32) for i in range(2)]
        st2 = mk.sb("st2", [128, 12], F32)
        mv2 = mk.sb("mv2", [128, 2], F32)
        rs2 = mk.sb("rs2", [128, 1], F32)
        xb = mk.sb("xb", [128, D], BF16)
        xT = mk.sb("xT", [128, 8, 128], BF16)
        gu2 = [mk.sb(f"gu{i}", [128, D], F32) for i in range(2)]
        gv2 = [mk.sb(f"gv{i}", [128, D], F32) for i in range(2)]
        vn2 = [mk.sb(f"vn{i}", [128, D], BF16) for i in range(2)]
        ygm = mk.sb("ygm", [128, D], BF16)
        ygf = mk.sb("ygf", [128, D], F32)
        yT = mk.sb("yT", [128, 8, 128], BF16)
        sg22 = [mk.sb(f"sg2{i}", [128, D], F32) for i in range(2)]
        hf_ = mk.sb("hf", [128, D], F32)
        hb = mk.sb("hb", [128, D], BF16)
        hT = mk.sb("hT", [128, 8, 128], BF16)
        xo = [mk.sb(f"xo{i}", [128, D], F32) for i in range(2)]
        st = mk.sb("st", [128, 12], F32)
        mv = mk.sb("mv", [128, 2], F32)
        rs = mk.sb("rs", [128, 1], F32)
        W0 = mk.ps("W0", [128, D], F32)
        W1 = mk.ps("W1", [128, D], F32)
        W2 = mk.ps("W2", [128, D], F32)
        pTb = mk.ps("pTb", [128, 8, 128], BF16)

        def v3b(b):
            return b.v(b.t[:].rearrange("p (h d) -> p h d", h=8))

        def stage1b(c):
            xt = xin[c % 3]
            gu, vn, sg2 = gu2[c % 2], vn2[c % 2], sg22[c % 2]
            make_xT(xt, xb, pTb, xT)
            yield
            proj1024(W1, xT, wv_)
            E(ACT, "activation", out=gv2[c % 2][:], in_=W1[:], func=AF.Gelu)
            yield
            proj1024(W0, xT, wu)
            E(ACT, "activation", out=gu[:], in_=W0[:], func=AF.Gelu)
            yield
            proj1024(W2, xT, wgg)
            E(ACT, "activation", out=sg2[:], in_=W2[:], func=AF.Sigmoid)

        def stage1bb(c):
            layer_norm(gv2[c % 2], vn2[c % 2][:], gg[:], gb[:], st, mv, rs)

        def stage2b(c):
            xt = xin[c % 3]
            h1 = h1in[c % 2]
            gu, vn, sg2 = gu2[c % 2], vn2[c % 2], sg22[c % 2]
            for g in range(8):
                MM(W0[:, g * 128:(g + 1) * 128], [(wcT[:, g, :], vn[:, g * 128:(g + 1) * 128])])
            E(DVE, "tensor_tensor", out=v3b(ygf), in0=v3b(W0),
              in1=View(bsp, bsp.t[:, :].unsqueeze(2).to_broadcast([128, 8, 128])), op=ALU.add)
            E(DVE, "tensor_tensor", out=ygm[:], in0=ygf[:], in1=gu[:], op=ALU.mult)
            yield
            for k in range(8):
                TR(pTb[:, k, :], ygm[:, k * 128:(k + 1) * 128], idb[:])
            E(DVE, "tensor_copy", out=yT[:], in_=pTb[:])
            proj1024(W1, yT, wpg)
            E(DVE, "tensor_tensor", out=hf_[:], in0=W1[:], in1=sg2[:], op=ALU.mult)
            E(DVE, "tensor_tensor", out=hb[:], in0=hf_[:], in1=h1[:], op=ALU.add)
            yield
            for k in range(8):
                TR(pTb[:, k, :], hb[:, k * 128:(k + 1) * 128], idb[:])
            E(DVE, "tensor_copy", out=hT[:], in_=pTb[:])
            yield
            proj1024(W2, hT, wout)
            xo_ = xo[c % 2]
            E(DVE, "scalar_tensor_tensor", out=xo_[:], in0=xt[:], scalar=ALPHA, in1=W2[:], op0=ALU.mult, op1=ALU.add)
            layer_norm(xo_, xo_[:], lg[:], lb[:], st2, mv2, rs2)
            SP.dma(X1t[c][:, :], xo_[:])

        for c0 in range(min(2, NT)):
            SP.dma(xin[c0][:], Xcur[c0][:, :])
        SP.dma(h1in[0][:], H1t[0][:, :])
        run_all(stage1b(0))
        stage1bb(0)
        for c in range(NT):
            if c + 1 < NT:
                SP.dma(h1in[(c + 1) % 2][:], H1t[c + 1][:, :])
                if c + 2 < NT:
                    SP.dma(xin[(c + 2) % 3][:], Xcur[c + 2][:, :])
                interleave(stage1b(c + 1), stage2b(c))
                stage1bb(c + 1)
            else:
                run_all(stage2b(c))
        mk.end_phase()
        if stop_after == "B":
            mk.finish()
            return nc

        mk.begin_phase()
        wq_ = mk.sb("wq", [128, 8, 1024], BF16)
        wo_ = mk.sb("wo", [128, 8, 1024], BF16)
        wk_ = mk.sb("wk", [128, 8, 1024], BF16)
        wvv = mk.sb("wvv", [128, 8, 1024], BF16)
        wload(POOL, wk_, wk, wk.t[l], 0, 1024)
        wload(POOL, wvv, wv, wv.t[l], 0, 1024)
        wload(POOL, wq_, wq, wq.t[l], 0, 1024)
        wload(POOL, wo_, wo, wo.t[l], 0, 1024)
        kT = mk.sb("kT", [128, 8, MEM], BF16)
        vtm = mk.sb("vtm", [128, 2, D], BF16)
        lg = mk.sb("lg", [128, D], F32)
        lb = mk.sb("lb", [128, D], F32)
        bc_load(lg, ln_gb.t[l, 1, 0:1, :])
        bc_load(lb, ln_gb.t[l, 1, 1:2, :])
        xin = [mk.sb(f"xin{i}", [128, D], F32) for i in range(3)]
        xb = mk.sb("xb", [128, D], BF16)
        xT = mk.sb("xT", [128, 8, 128], BF16)
        qT = mk.sb("qT", [128, 8, 128], BF16)
        Pf = mk.sb("Pf", [128, 4, MEM], F32)
        Pn2 = [mk.sb(f"Pn{i}", [128, 4, MEM], BF16) for i in range(2)]
        PT = mk.sb("PT", [128, 8, 128], BF16)
        oT = mk.sb("oT", [128, 8, 128], BF16)
        mx = mk.sb("mx", [128, 12], F32)
        xo = [mk.sb(f"xo{i}", [128, D], F32) for i in range(2)]
        st = mk.sb("st", [128, 12], F32)
        mv = mk.sb("mv", [128, 2], F32)
        rs = mk.sb("rs", [128, 1], F32)
        W0 = mk.ps("W0", [128, D], F32)
        W1 = mk.ps("W1", [128, D], F32)
        W2 = mk.ps("W2", [128, D], F32)
        pTb = mk.ps("pTb", [128, 8, 128], BF16)
        for cc in range(8):
            MM(W0[:, (cc % 4) * 256:(cc % 4) * 256 + 256],
               [(wk_[:, k, cc * 128:(cc + 1) * 128], memT[:, k, :]) for k in range(8)])
            if cc % 4 == 3:
                E(ACT, "activation", out=kT[:, cc - 3:cc + 1, :],
                  in_=W0.v(W0.t[:].rearrange("p (a m) -> p a m", a=4)), func=AF.Copy)
        for a in range(2):
            proj1024(W1, memT.v(memT.t[:, :, a * 128:(a + 1) * 128]), wvv)
            E(ACT, "activation", out=vtm[:, a, :], in_=W1[:], func=AF.Copy)

        def stage1c(c):
            xt = xin[c % 3]
            Pn = Pn2[c % 2]
            make_xT(xt, xb, pTb, xT)
            for qc in range(8):
                MM(W0[:, qc * 128:(qc + 1) * 128],
                   [(wq_[:, k, qc * 128:(qc + 1) * 128], xT[:, k, :]) for k in range(8)])
            E(ACT, "activation", out=qT[:], in_=W0.v(W0.t[:].rearrange("p (a m) -> p a m", a=8)),
              func=AF.Copy, scale=1.0 / 16.0)
            yield
            for h in range(4):
                MM(W1[:, h * 256:(h + 1) * 256],
                   [(qT[:, 2 * h + dc, :], kT[:, 2 * h + dc, :]) for dc in range(2)])

        def stage1cb(c):
            Pn = Pn2[c % 2]
            W1v = W1.v(W1.t[:].rearrange("p (h m) -> p h m", h=4))
            E(DVE, "tensor_reduce", out=mx[:, 0:4], in_=W1v, axis=AX.X, op=ALU.max)
            E(DVE, "tensor_scalar", out=mx[:, 4:8], in0=mx[:, 0:4], scalar1=-1.0, scalar2=None, op0=ALU.mult)
            for h in range(4):
                E(ACT, "activation", out=Pf[:, h, :], in_=W1[:, h * 256:(h + 1) * 256], func=AF.Exp,
                  bias=mx[:, 4 + h:5 + h], accum_out=mx[:, 8 + h:9 + h])
            E(DVE, "reciprocal", out=mx[:, 0:4], in_=mx[:, 8:12])
            E(DVE, "tensor_tensor", out=Pn[:], in0=Pf[:],
              in1=View(mx, mx.t[:, 0:4].unsqueeze(2).to_broadcast([128, 4, MEM])), op=ALU.mult)

        def stage2c(c):
            xt = xin[c % 3]
            Pn = Pn2[c % 2]
            for h in range(4):
                for mc in range(2):
                    TR(pTb[:, 2 * h + mc, :], Pn[:, h, mc * 128:(mc + 1) * 128], idb[:])
            E(DVE, "tensor_copy", out=PT[:], in_=pTb[:])
            yield
            for oc in range(8):
                h = oc // 2
                MM(W0[:, oc * 128:(oc + 1) * 128],
                   [(vtm[:, mc, oc * 128:(oc + 1) * 128], PT[:, 2 * h + mc, :]) for mc in range(2)])
            E(ACT, "activation", out=oT[:], in_=W0.v(W0.t[:].rearrange("p (a m) -> p a m", a=8)), func=AF.Copy)
            yield
            proj1024(W2, oT, wo_)
            xo_ = xo[c % 2]
            E(DVE, "scalar_tensor_tensor", out=xo_[:], in0=xt[:], scalar=ALPHA, in1=W2[:], op0=ALU.mult, op1=ALU.add)
            layer_norm(xo_, xo_[:], lg[:], lb[:], st, mv, rs)
            SP.dma(X2t[c][:, :], xo_[:])

        for c0 in range(min(2, NT)):
            SP.dma(xin[c0][:], X1t[c0][:, :])
        run_all(stage1c(0))
        stage1cb(0)
        for c in range(NT):
            if c + 1 < NT:
                if c + 2 < NT:
                    SP.dma(xin[(c + 2) % 3][:], X1t[c + 2][:, :])
                interleave(stage1c(c + 1), stage2c(c))
                stage1cb(c + 1)
            else:
                run_all(stage2c(c))
        mk.end_phase()
        if stop_after == "C":
            mk.finish()
            return nc

        TB = min(S, 1024)
        NB = S // TB
        TT = TB // 128
        SUBN = min(TB, 512)
        NSUB = TB // SUBN
        _NEXP = int(_os.environ.get('KDBG_NEXP', NE))
        mk.begin_phase()
        lg = mk.sb("lg", [128, D], F32)
        lb = mk.sb("lb", [128, D], F32)
        bc_load(lg, ln_gb.t[l, 2, 0:1, :])
        bc_load(lb, ln_gb.t[l, 2, 1:2, :])
        wr = mk.sb("wr", [128, 8, NE], F32)
        SP.dma(wr[:], dsrc(w_router, w_router.t[l].rearrange("(k p) n -> p k n", p=128)))
        br = mk.sb("br", [128, NE], F32)
        bc_load(br, b_router.t[l:l + 1, :])
        bgu = mk.sb("bgu", [128, NE, 16], F32)
        SP.dma(bgu[:], b_guT[l, :, :, :])
        E(DVE, "tensor_scalar", out=bgu[:, :, 0:8], in0=bgu[:, :, 0:8], scalar1=-1.0, scalar2=7.0, op0=ALU.mult, op1=ALU.add)
        E(DVE, "tensor_scalar", out=bgu[:, :, 8:16], in0=bgu[:, :, 8:16], scalar1=7.0, scalar2=None, op0=ALU.add)
        bdn = mk.sb("bdn", [128, D], F32)
        POOL.op("memset", [bdn[:]], [], ap=bdn[:], constant=0.0)
        SP.dma(bdn[0:NE, :], b_down[l, :, :])
        gpad_ = [mk.sb(f"gpad{i}", [128, 128], F32) for i in range(2)]
        for gp_ in gpad_:
            POOL.op("memset", [gp_[:]], [], ap=gp_[:], constant=0.0)
        wgu_b = [mk.sb(f"wgu{i}", [128, 8, 2 * D], BF16) for i in range(2)]
        wdn_b = [mk.sb(f"wdn{i}", [128, 8, D], BF16) for i in range(2)]
        acc = [mk.sb(f"acc{i}", [128, D], F32) for i in range(TT)]
        xTm = mk.sb("xTm", [128, 8, TB], BF16)
        lgt_ = [mk.sb(f"lgt{i}", [128, NE], F32) for i in range(2)]
        m8_ = [mk.sb(f"m8{i}", [128, 8], F32) for i in range(2)]
        msk_ = [mk.sb(f"msk{i}", [128, NE], F32) for i in range(2)]
        ex_ = [mk.sb(f"ex{i}", [128, NE], F32) for i in range(2)]
        sm2_ = [mk.sb(f"sm2{i}", [128, 4], F32) for i in range(2)]
        gates = mk.sb("gates", [128, TT, NE], F32)
        gT_ = [mk.sb(f"gT{i}", [128, 128], F32) for i in range(2)]
        actT = [mk.sb(f"actT{i}", [128, 8, SUBN], BF16) for i in range(2)]
        NTMP = 3
        tmps = [[mk.sb(f"tm{i}_{q}", [128, SUBN], F32) for q in range(3)] for i in range(NTMP)]
        st = mk.sb("st", [128, 12], F32)
        mv = mk.sb("mv", [128, 2], F32)
        rs = mk.sb("rs", [128, 1], F32)
        HGg = [mk.ps(f"HGg{i}", [128, 512], F32) for i in range(3)]
        HGu = [mk.ps(f"HGu{i}", [128, 512], F32) for i in range(3)]
        YPh = [mk.ps(f"YPh{i}", [128, 512], F32) for i in range(2)]

        def load_expert(e, slot):
            src = w_gu.t[l, e].rearrange("(k p) n -> p k n", p=128)
            for q in range(4):
                POOL.dma(wgu_b[slot][:, 2 * q:2 * q + 2, :], dsrc(w_gu, src[:, 2 * q:2 * q + 2, :]))
            src2 = w_down.t[l, e].rearrange("(k p) n -> p k n", p=128)
            for q in range(2):
                POOL.dma(wdn_b[slot][:, 4 * q:4 * q + 4, :], dsrc(w_down, src2[:, 4 * q:4 * q + 4, :]))

        units = [(e, sb_) for e in range(_NEXP) for sb_ in range(NSUB)]
        pstate = {"pi": 0}

        def emit_hgu(u, blk):
            e, sb_ = units[u]
            wg_ = wgu_b[e % 2]
            t0 = sb_ * SUBN
            aT = actT[u % 2]
            for j in range(8):
                pi = pstate["pi"]
                pstate["pi"] += 1
                gb, ub = HGg[pi % 3], HGu[pi % 3]
                rg, b_, r_ = tmps[pi % NTMP]
                MM(gb[:, 0:SUBN], [(wg_[:, k, j * 128:(j + 1) * 128], xTm[:, k, t0:t0 + SUBN]) for k in range(8)])
                MM(ub[:, 0:SUBN], [(wg_[:, k, D + j * 128:D + (j + 1) * 128], xTm[:, k, t0:t0 + SUBN]) for k in range(8)])
                E(ACT, "activation", out=rg[:], in_=gb[:, 0:SUBN], func=AF.Relu, scale=-1.0, bias=bgu[:, e, j:j + 1])
                E(ACT, "activation", out=r_[:], in_=ub[:, 0:SUBN], func=AF.Relu, bias=bgu[:, e, 8 + j:9 + j])
                E(ACT, "activation", out=b_[:], in_=rg[:], func=AF.Sigmoid, scale=-SWA, bias=c7a[:])
                E(ACT, "activation", out=r_[:], in_=r_[:], func=AF.Relu, scale=-1.0, bias=c14[:])
                E(DVE, "scalar_tensor_tensor", out=rg[:], in0=rg[:], scalar=7.0, in1=b_[:], op0=ALU.subtract, op1=ALU.mult)
                E(DVE, "scalar_tensor_tensor", out=aT[:, j, :], in0=r_[:], scalar=8.0, in1=rg[:], op0=ALU.subtract, op1=ALU.mult)

        def emit_down(u, blk):
            e, sb_ = units[u]
            wd_ = wdn_b[e % 2]
            aT = actT[u % 2]
            for t4 in range(SUBN // 128):
                tt = sb_ * (SUBN // 128) + t4
                for hf in range(2):
                    yp = YPh[hf]
                    MM(yp[:, :], [(aT[:, j, t4 * 128:(t4 + 1) * 128], wd_[:, j, hf * 512:(hf + 1) * 512]) for j in range(8)])
                    E(DVE, "scalar_tensor_tensor", out=acc[tt][:, hf * 512:(hf + 1) * 512], in0=yp[:, :],
                      scalar=gates[:, tt, e:e + 1], in1=acc[tt][:, hf * 512:(hf + 1) * 512], op0=ALU.mult, op1=ALU.add)

        SWA = 1.702
        c7a = mk.sb("c7a", [128, 1], F32)
        c14 = mk.sb("c14", [128, 1], F32)
        POOL.op("memset", [c7a[:]], [], ap=c7a[:], constant=7.0 * SWA)
        POOL.op("memset", [c14[:]], [], ap=c14[:], constant=14.0)

        for blk in range(NB):
            if _NEXP > 0:
                load_expert(0, 0)
            if _NEXP > 1:
                load_expert(1, 1)
            for tt in range(TT):
                SP.dma(acc[tt][:], X2t[blk * TT + tt][:, :])
            for tt in range(TT):
                xt = acc[tt]
                p2 = tt % 2
                PA, PB = HGg[tt % 3], HGu[tt % 3]
                xa, xb_ = tmps[tt % 3][0], tmps[tt % 3][1]
                xa3 = xa.v(xa.t[:].rearrange("p (a m) -> p a m", a=4))
                xb3 = xb_.v(xb_.t[:].rearrange("p (a m) -> p a m", a=4))
                lgt, m8, msk, ex, sm2, gT, gpad = lgt_[p2], m8_[p2], msk_[p2], ex_[p2], sm2_[p2], gT_[p2], gpad_[p2]
                for k in range(8):
                    dstp = PA if k < 4 else PB
                    TR(dstp[:, (k % 4) * 128:(k % 4 + 1) * 128], xt[:, k * 128:(k + 1) * 128], identf)
                E(ACT, "activation", out=xa3, in_=PA.v(PA.t[:].rearrange("p (a m) -> p a m", a=4)), func=AF.Copy)
                E(ACT, "activation", out=xb3, in_=PB.v(PB.t[:].rearrange("p (a m) -> p a m", a=4)), func=AF.Copy)
                E(DVE, "tensor_copy", out=xTm[:, 0:4, tt * 128:(tt + 1) * 128], in_=xa3)
                E(DVE, "tensor_copy", out=xTm[:, 4:8, tt * 128:(tt + 1) * 128], in_=xb3)
                RP = YPh[p2]
                MM(RP[:, 0:NE], [((xa3 if k < 4 else xb3)[:, k % 4, :], wr[:, k, :]) for k in range(8)])
                E(DVE, "tensor_tensor", out=lgt[:], in0=RP[:, 0:NE], in1=br[:], op=ALU.add)
                E(DVE, "max", out=m8[:], in_=lgt[:])
                E(DVE, "tensor_scalar", out=msk[:], in0=lgt[:], scalar1=m8[:, 3:4], scalar2=None, op0=ALU.is_ge)
                E(DVE, "tensor_scalar", out=sm2[:, 0:1], in0=m8[:, 0:1], scalar1=-1.0, scalar2=None, op0=ALU.mult)
                E(ACT, "activation", out=ex[:], in_=lgt[:], func=AF.Exp, bias=sm2[:, 0:1])
                E(DVE, "tensor_tensor", out=ex[:], in0=ex[:], in1=msk[:], op=ALU.mult)
                E(DVE, "reduce_sum", out=sm2[:, 1:2], in_=ex[:], axis=AX.X)
                E(DVE, "reciprocal", out=sm2[:, 2:3], in_=sm2[:, 1:2])
                E(DVE, "tensor_scalar", out=gates[:, tt, :], in0=ex[:], scalar1=sm2[:, 2:3], scalar2=None, op0=ALU.mult)
                E(DVE, "tensor_copy", out=gpad[:, 0:NE], in_=gates[:, tt, :])
                TR(RP[:, 128:256], gpad[:], identf)
                E(ACT, "activation", out=gT[:], in_=RP[:, 128:256], func=AF.Copy)
                MM(PA[:, :], [(gT[:], bdn[:, 0:512])])
                MM(PB[:, :], [(gT[:], bdn[:, 512:1024])])
                E(DVE, "scalar_tensor_tensor", out=xt[:, 0:512], in0=xt[:, 0:512], scalar=ALPHA, in1=PA[:, :],
                  op0=ALU.mult, op1=ALU.add)
                E(DVE, "scalar_tensor_tensor", out=xt[:, 512:1024], in0=xt[:, 512:1024], scalar=ALPHA, in1=PB[:, :],
                  op0=ALU.mult, op1=ALU.add)
            NU = len(units)
            if NU > 0:
                emit_hgu(0, blk)
            for u in range(NU):
                if u + 1 < NU:
                    emit_hgu(u + 1, blk)
                emit_down(u, blk)
                e, sb_ = units[u]
                if sb_ == NSUB - 1 and e + 2 < _NEXP:
                    load_expert(e + 2, e % 2)
            for tt in range(TT):
                layer_norm_view(acc[tt][:], acc[tt], lg, lb, st, mv, rs)
                SP.dma(X3t[blk * TT + tt][:, :], acc[tt][:])
        mk.end_phase()

        Xcur = X3t

    mk.finish()
    return nc


def layer_norm_view(at, o_, lg, lb, st, mv, rs):
    mk = o_.mk
    DVE, POOL, ACT = mk.DVE, mk.POOL, mk.ACT
    a0 = View(at.buf, at.ap[:, 0:512])
    a1 = View(at.buf, at.ap[:, 512:1024])
    E(DVE, "bn_stats", out=st[:, 0:6], in_=a0)
    E(DVE, "bn_stats", out=st[:, 6:12], in_=a1)
    E(DVE, "bn_aggr", out=mv[:], in_=st[:])
    E(ACT, "activation", out=rs[:], in_=mv[:, 1:2], func=AF.Sqrt, bias=o_.mk.epsb[:], scale=1.0)
    E(DVE, "reciprocal", out=rs[:], in_=rs[:])
    E(DVE, "tensor_scalar", out=o_[:], in0=at, scalar1=mv[:, 0:1], scalar2=rs[:], op0=ALU.subtract, op1=ALU.mult)
    E(DVE, "tensor_tensor", out=o_[:], in0=o_[:], in1=lg[:], op=ALU.mult)
    E(DVE, "tensor_tensor", out=o_[:], in0=o_[:], in1=lb[:], op=ALU.add)


def host_inputs(inp, b, S):
    f = np.float32
    L = inp["w_in"].shape[0]
    tri = np.triu(np.ones((128, 128), f))
    ustr = np.tril(np.ones((128, 128), f), -1)
    cf = np.stack([np.eye(128, dtype=f), tri, ustr, np.ones((128, 128), f)], axis=1)
    m = {
        "x": np.ascontiguousarray(inp["x"][b, :S]),
        "mem": np.ascontiguousarray(inp["mem"][b]),
        "ln0": np.stack([inp["ln0_g"], inp["ln0_b"]]).astype(f),
        "w_in": inp["w_in"],
        "conv_w": np.ascontiguousarray(inp["conv_w"].reshape(L, 4, 16, 128).transpose(0, 3, 2, 1)),
        "conv_b": np.ascontiguousarray(inp["conv_b"].reshape(L, 16, 128).transpose(0, 2, 1)),
        "vec16": np.ascontiguousarray(np.stack([inp["dt_bias"], inp["a_log"], inp["d_skip"]], axis=1)),
        "ssd_norm_w": inp["ssd_norm_w"],
        "gm_ln": np.ascontiguousarray(np.stack([inp["gm_ln_g"], inp["gm_ln_b"]], axis=1)),
        "w_spT": np.ascontiguousarray(inp["w_sp"].transpose(0, 3, 1, 2)),
        "b_spT": np.ascontiguousarray(inp["b_sp"].transpose(0, 2, 1)),
        "p_ssd": inp["p_ssd"], "p_gm": inp["p_gm"], "w_out": inp["w_out"],
        "wq": inp["wq"], "wk": inp["wk"], "wv": inp["wv"], "wo": inp["wo"],
        "w_router": inp["w_router"], "b_router": inp["b_router"],
        "w_gu": inp["w_gu"],
        "b_guT": np.ascontiguousarray(inp["b_gu"].reshape(L, NE, 16, 128).transpose(0, 3, 1, 2)),
        "w_down": inp["w_down"], "b_down": inp["b_down"],
        "ln_gb": np.ascontiguousarray(np.stack([inp["ln_g"], inp["ln_b"]], axis=2)),
        "cf32": np.ascontiguousarray(cf),
        "cidb": np.eye(128, dtype=f).astype(ml_dtypes.bfloat16),
    }
    return {k: np.ascontiguousarray(np.asarray(v)) for k, v in m.items()}


_NC_CACHE = {}


def kernel(**inputs):
    inp = {k: np.asarray(v) for k, v in inputs.items()}
    B, S, _ = inp["x"].shape
    L = inp["w_in"].shape[0]
    key = (S, L)
    if key not in _NC_CACHE:
        _NC_CACHE[key] = build(S, L)
    nc = _NC_CACHE[key]
    in_maps = [host_inputs(inp, b, S) for b in range(B)]
    res = run_bass_kernel_spmd(nc, in_maps, core_ids=list(range(B)))
    out = np.stack([np.asarray(r["y_out"]) for r in res.results], axis=0)
    return out.astype(np.float32)
```

```python
import numpy as np
import os as _os
from contextlib import ExitStack
import ml_dtypes
import concourse.bass as bass
import concourse.mybir as mybir
from concourse.bass_utils import run_bass_kernel_spmd

F32 = mybir.dt.float32
BF16 = mybir.dt.bfloat16
AF = mybir.ActivationFunctionType
ALU = mybir.AluOpType
AX = mybir.AxisListType

D = 1024
MEM = 256
NE = 32
ALPHA = float(4 ** 0.25)
EPS = 1e-5
OFF_Z, OFF_XBC, OFF_DT, OFF_U, OFF_V, OFF_G = 0, 1024, 3072, 3088, 4112, 5136


class View:
    __slots__ = ("buf", "ap")

    def __init__(self, buf, ap):
        self.buf = buf
        self.ap = ap

    def __getitem__(self, k):
        return View(self.buf, self.ap[k])


class Buf:
    def __init__(self, mk, t, name, is_sb=True):
        self.mk = mk
        self.t = t
        self.name = name
        self.w = None
        self.r = {}
        self.dsem = None
        self.is_sb = is_sb

    def __getitem__(self, k):
        return View(self, self.t[k])

    def v(self, ap):
        return View(self, ap)


class Eng:
    def __init__(self, mk, eng, name):
        self.mk = mk
        self.eng = eng
        self.name = name
        self.sem = mk.new_sem("e_" + name)
        self.cnt = 0
        self.seen = {}

    def wait(self, ev):
        if ev is None:
            return
        sem, val = ev[0], ev[1]
        if self.seen.get(sem, 0) >= val:
            return
        self.eng.wait_ge(sem, val)
        self.seen[sem] = val

    def deps(self, outs, ins):
        for v in ins:
            self.wait(v.buf.w)
            if getattr(v.buf, "is_psum", False):
                for sem, (val, en) in v.buf.r.items():
                    if en != self.name:
                        self.wait((sem, val))
        for v in outs:
            b = v.buf
            if b.w is not None and b.w[2] != self.name:
                self.wait(b.w)
            for sem, (val, en) in b.r.items():
                if en != self.name:
                    self.wait((sem, val))

    def done(self, instr, outs, ins):
        self.cnt += 1
        instr.then_inc(self.sem, 1)
        for v in ins:
            v.buf.r[self.sem] = (self.cnt, self.name)
        for v in outs:
            v.buf.w = (self.sem, self.cnt, self.name)
            v.buf.r = {}

    def op(self, fname, outs, ins, **kw):
        self.deps(outs, ins)
        k = {kk: (x.ap if isinstance(x, View) else x) for kk, x in kw.items()}
        instr = getattr(self.eng, fname)(**k)
        self.done(instr, outs, ins)
        return instr

    def dma(self, out, in_, **kw):
        b = out.buf if out.buf.is_sb else in_.buf
        if b.dsem is None:
            b.dsem = self.mk.take_dsem(self.name == "pool")
            b.dsem_sw = (self.name == "pool")
        assert b.dsem_sw == (self.name == "pool")
        self.deps([out], [in_])
        instr = self.eng.dma_start(out=out.ap, in_=in_.ap, **kw)
        b.dsem[1] += 16
        instr.then_inc(b.dsem[0], 16)
        in_.buf.r[b.dsem[0]] = (b.dsem[1], "dma")
        out.buf.w = (b.dsem[0], b.dsem[1], "dma")
        out.buf.r = {}
        self.mk.dma_events[b.dsem[0]] = b.dsem[1]
        return instr


class MK:
    def __init__(self, nc):
        self.nc = nc
        self.es = ExitStack()
        self.nsem = 0
        self.free_dsems = []
        self.free_dsems_sw = []
        self.dma_events = {}
        self.PE = Eng(self, nc.tensor, "pe")
        self.ACT = Eng(self, nc.scalar, "act")
        self.DVE = Eng(self, nc.vector, "dve")
        self.POOL = Eng(self, nc.gpsimd, "pool")
        self.SP = Eng(self, nc.sync, "sp")
        self.engs = [self.PE, self.ACT, self.DVE, self.POOL, self.SP]
        self.phase_stack = None
        self.phase_bufs = []
        self.uid = 0

    def new_sem(self, name):
        self.nsem += 1
        return self.es.enter_context(self.nc.semaphore(f"{name}_{self.nsem}"))

    def take_dsem(self, sw):
        fl = self.free_dsems_sw if sw else self.free_dsems
        if fl:
            return fl.pop()
        return [self.new_sem("dsw" if sw else "d"), 0]

    def _alloc(self, fn, name, shape, dt, persist):
        st = self.es if (persist or self.phase_stack is None) else self.phase_stack
        self.uid += 1
        t = st.enter_context(fn(f"{name}_{self.uid}", list(shape), dt))
        b = Buf(self, t, name, True)
        if st is not self.es:
            self.phase_bufs.append(b)
        return b

    def sb(self, name, shape, dt, persist=False):
        return self._alloc(self.nc.sbuf_tensor, name, shape, dt, persist)

    def ps(self, name, shape, dt, persist=False):
        b = self._alloc(self.nc.psum_tensor, name, shape, dt, persist)
        b.is_psum = True
        return b

    def dram(self, name, shape, dt, kind="Internal"):
        t = self.nc.dram_tensor(name, list(shape), dt, kind=kind).ap()
        return Buf(self, t, name, False)

    def barrier(self):
        for sem, val in self.dma_events.items():
            self.SP.wait((sem, val))
        self.dma_events = {}
        instr = self.SP.eng.nop()
        self.SP.cnt += 1
        instr.then_inc(self.SP.sem, 1)
        for e in self.engs:
            for o in self.engs:
                if o is not e and o.cnt > 0:
                    e.wait((o.sem, o.cnt))

    def begin_phase(self):
        self.phase_stack = ExitStack()
        self.phase_bufs = []

    def end_phase(self):
        self.barrier()
        for b in self.phase_bufs:
            if b.dsem is not None:
                (self.free_dsems_sw if b.dsem_sw else self.free_dsems).append(b.dsem)
                b.dsem = None
        self.phase_stack.close()
        self.phase_stack = None
        self.phase_bufs = []

    def finish(self):
        self.barrier()
        self.es.close()


def E(eng, fname, out=None, accum_out=None, **kw):
    outs = [out] + ([accum_out] if accum_out is not None else [])
    ins = [v for v in kw.values() if isinstance(v, View)]
    kk = dict(kw)
    kk["out"] = out
    if accum_out is not None:
        kk["accum_out"] = accum_out
    return eng.op(fname, outs, ins, **kk)


def tiles_of(mk, dbuf, nt):
    return [Buf(mk, dbuf.t[i * 128:(i + 1) * 128, :], f"{dbuf.name}_{i}", False) for i in range(nt)]


def build(S, depth, dbg=False, stop_after=None):
    NT = S // 128
    nc = bass.Bass("TRN2", target_bir_lowering=False)
    mk = MK(nc)
    PE, ACT, DVE, POOL, SP = mk.PE, mk.ACT, mk.DVE, mk.POOL, mk.SP
    L = depth

    def din(name, shape, dt=F32):
        return mk.dram(name, shape, dt, kind="ExternalInput")

    x_in = din("x", [S, D])
    mem_in = din("mem", [MEM, D])
    ln0 = din("ln0", [2, D])
    w_in = din("w_in", [L, D, 7184])
    conv_w = din("conv_w", [L, 128, 16, 4])
    conv_b = din("conv_b", [L, 128, 16])
    vec16 = din("vec16", [L, 3, 16])
    ssd_norm_w = din("ssd_norm_w", [L, D])
    gm_ln = din("gm_ln", [L, 2, D])
    w_spT = din("w_spT", [L, 128, 8, 128])
    b_spT = din("b_spT", [L, 128, 8])
    p_ssd = din("p_ssd", [L, D, D])
    p_gm = din("p_gm", [L, D, D])
    w_out = din("w_out", [L, D, D])
    wq = din("wq", [L, D, D])
    wk = din("wk", [L, D, D])
    wv = din("wv", [L, D, D])
    wo = din("wo", [L, D, D])
    w_router = din("w_router", [L, D, NE])
    b_router = din("b_router", [L, NE])
    w_gu = din("w_gu", [L, NE, D, 2 * D])
    b_guT = din("b_guT", [L, 128, NE, 16])
    w_down = din("w_down", [L, NE, D, D])
    b_down = din("b_down", [L, NE, D])
    ln_gb = din("ln_gb", [L, 3, 2, D])
    cf32 = din("cf32", [128, 4, 128])
    cidb = din("cidb", [128, 128], BF16)
    out_d = mk.dram("y_out", [S, D], F32, kind="ExternalOutput")

    skind = "ExternalOutput" if dbg else "Internal"
    X0 = mk.dram("X0", [S, D], F32, kind=skind)
    scr = []
    for l in range(L):
        d = {}
        for nm in ("H1", "X1", "X2"):
            d[nm] = mk.dram(f"{nm}_{l}", [S, D], F32, kind=skind)
        d["X3"] = out_d if l == L - 1 else mk.dram(f"X3_{l}", [S, D], F32, kind=skind)
        scr.append(d)

    cf = mk.sb("cf", [128, 4, 128], F32, persist=True)
    idb = mk.sb("idb", [128, 128], BF16, persist=True)
    memT = mk.sb("memT", [128, 8, MEM], BF16, persist=True)
    SP.dma(cf[:], cf32[:, :, :])
    SP.dma(idb[:], cidb[:, :])
    epsb = mk.sb("epsb", [128, 1], F32, persist=True)
    mk.epsb = epsb
    POOL.op("memset", [epsb[:]], [], ap=epsb[:], constant=EPS)
    identf = cf[:, 0, :]
    tri = cf[:, 1, :]
    ustr = cf[:, 2, :]
    ones = cf[:, 3, :]

    def MM(out, pairs):
        reads = [v for p in pairs for v in p]
        PE.deps([out], reads)
        n = len(pairs)
        ins = None
        for i, (l_, r_) in enumerate(pairs):
            ins = nc.tensor.matmul(out.ap, lhsT=l_.ap, rhs=r_.ap, start=(i == 0), stop=(i == n - 1))
        PE.done(ins, [out], reads)

    def TR(out, in_, ident):
        E(PE, "transpose", out=out, in_=in_, identity=ident)

    def bc_load(buf, src_row_ap):
        SP.dma(buf[:], Buf(mk, src_row_ap.partition_broadcast(128), "bc", False)[:])

    def dsrc(dbuf, ap):
        return View(dbuf, ap)

    def wload(eng, dst, dbuf, ap2d, c0, c1):
        v = ap2d.rearrange("(k p) n -> p k n", p=128)
        eng.dma(dst[:], dsrc(dbuf, v[:, :, c0:c1]))

    def layer_norm(t, dst, g_bc, b_bc, st, mv, rs):
        E(DVE, "bn_stats", out=st[:, 0:6], in_=t[:, 0:512])
        E(DVE, "bn_stats", out=st[:, 6:12], in_=t[:, 512:1024])
        E(DVE, "bn_aggr", out=mv[:], in_=st[:])
        E(ACT, "activation", out=rs[:], in_=mv[:, 1:2], func=AF.Sqrt, bias=epsb[:], scale=1.0)
        E(DVE, "reciprocal", out=rs[:], in_=rs[:])
        E(DVE, "tensor_scalar", out=t[:], in0=t[:], scalar1=mv[:, 0:1], scalar2=rs[:], op0=ALU.subtract, op1=ALU.mult)
        E(DVE, "tensor_tensor", out=t[:], in0=t[:], in1=g_bc, op=ALU.mult)
        E(DVE, "tensor_tensor", out=dst, in0=t[:], in1=b_bc, op=ALU.add)

    def make_xT(xt, xb, pTb, xT):
        E(ACT, "activation", out=xb[:], in_=xt[:], func=AF.Copy)
        for k in range(8):
            TR(pTb[:, k, :], xb[:, k * 128:(k + 1) * 128], idb[:])
        E(DVE, "tensor_copy", out=xT[:], in_=pTb[:])

    def proj1024(Wps, lhsT3, w3):
        for hf in range(2):
            MM(Wps[:, hf * 512:(hf + 1) * 512],
               [(lhsT3[:, k, :], w3[:, k, hf * 512:(hf + 1) * 512]) for k in range(8)])

    def run_all(gen):
        for _ in gen:
            pass

    def interleave(ga, gb):
        da = db = False
        while not (da and db):
            if not db:
                try:
                    next(gb)
                except StopIteration:
                    db = True
            if not da:
                try:
                    next(ga)
                except StopIteration:
                    da = True


    X0t = tiles_of(mk, X0, NT)
    xin_t = tiles_of(mk, x_in, NT)
    mk.begin_phase()
    g0 = mk.sb("g0", [128, D], F32)
    b0 = mk.sb("b0", [128, D], F32)
    bc_load(g0, ln0.t[0:1, :])
    bc_load(b0, ln0.t[1:2, :])
    xin = [mk.sb(f"xin{i}", [128, D], F32) for i in range(3)]
    st = mk.sb("st", [128, 12], F32)
    mv = mk.sb("mv", [128, 2], F32)
    rs = mk.sb("rs", [128, 1], F32)
    mt = mk.sb("mt", [128, 2, D], F32)
    mtb = mk.sb("mtb", [128, 2, D], BF16)
    pTb = mk.ps("pTb", [128, 8, 128], BF16)
    SP.dma(mt[:], dsrc(mem_in, mem_in.t.rearrange("(a p) d -> p a d", p=128)))
    E(ACT, "activation", out=mtb[:], in_=mt[:], func=AF.Copy)
    for a in range(2):
        for k in range(8):
            TR(pTb[:, k, :], mtb[:, a, k * 128:(k + 1) * 128], idb[:])
        E(DVE, "tensor_copy", out=memT[:, :, a * 128:(a + 1) * 128], in_=pTb[:])
    SP.dma(xin[0][:], xin_t[0][:, :])
    for i in range(NT):
        if i + 1 < NT:
            SP.dma(xin[(i + 1) % 3][:], xin_t[i + 1][:, :])
        xt = xin[i % 3]
        layer_norm(xt, xt[:], g0[:], b0[:], st, mv, rs)
        SP.dma(X0t[i][:, :], xt[:])
    mk.end_phase()
    if stop_after == "0":
        mk.finish()
        return nc

    Xcur = X0t
    for l in range(L):
        H1t = tiles_of(mk, scr[l]["H1"], NT)
        X1t = tiles_of(mk, scr[l]["X1"], NT)
        X2t = tiles_of(mk, scr[l]["X2"], NT)
        X3t = tiles_of(mk, scr[l]["X3"], NT)
        win2 = w_in.t[l]

        mk.begin_phase()
        wz = mk.sb("wz", [128, 8, 1024], BF16)
        wxbc = mk.sb("wxbc", [128, 8, 2048], BF16)
        wdt = mk.sb("wdt", [128, 8, 16], BF16)
        wg = mk.sb("wg", [128, 8, 1024], BF16)
        wps = mk.sb("wps", [128, 8, 1024], BF16)
        wload(POOL, wxbc, w_in, win2, OFF_XBC, OFF_XBC + 2048)
        wload(POOL, wz, w_in, win2, OFF_Z, OFF_Z + 1024)
        wload(POOL, wg, w_in, win2, OFF_G, OFF_G + 1024)
        wload(POOL, wdt, w_in, win2, OFF_DT, OFF_DT + 16)
        wload(POOL, wps, p_ssd, p_ssd.t[l], 0, 1024)
        cw = mk.sb("cw", [128, 16, 4], F32)
        cb = mk.sb("cb", [128, 16], F32)
        SP.dma(cw[:], conv_w[l, :, :, :])
        SP.dma(cb[:], conv_b[l, :, :])
        v16 = mk.sb("v16", [128, 3, 16], F32)
        for j in range(3):
            SP.dma(v16[:, j, :], Buf(mk, vec16.t[l, j:j + 1, :].partition_broadcast(128), "bc", False)[:])
        normw = mk.sb("normw", [128, D], F32)
        bc_load(normw, ssd_norm_w.t[l:l + 1, :])
        a_bc = mk.sb("a_bc", [128, 16], F32)
        E(ACT, "activation", out=a_bc[:], in_=v16[:, 1, :], func=AF.Exp)
        E(DVE, "tensor_scalar", out=a_bc[:], in0=a_bc[:], scalar1=-1.0, scalar2=None, op0=ALU.mult)
        dtb = v16[:, 0, :]
        dsk3 = v16.v(v16.t[:, 2, :].unsqueeze(2).to_broadcast([128, 16, 64]))

        xin = [mk.sb(f"xin{i}", [128, D], F32) for i in range(2)]
        xb = mk.sb("xb", [128, D], BF16)
        xT = mk.sb("xT", [128, 8, 128], BF16)
        sz2 = [mk.sb(f"sz{i}", [128, D], F32) for i in range(2)]
        sg2_ = [mk.sb(f"sg{i}", [128, D], F32) for i in range(2)]
        rawA = mk.sb("rawA", [128, 8, 131], F32)
        rawB = mk.sb("rawB", [128, 8, 131], F32)
        cvA = mk.sb("cvA", [128, 8, 128], F32)
        cvB = mk.sb("cvB", [128, 8, 128], F32)
        xsf2 = [mk.sb(f"xsf{i}", [128, 8, 128], F32) for i in range(2)]
        bcb2 = [mk.sb(f"bcb{i}", [128, 8, 128], BF16) for i in range(2)]
        ctA = mk.sb("ctA", [128, 8, 128], F32)
        ctB = mk.sb("ctB", [128, 8, 128], F32)
        xs_tm = mk.sb("xs_tm", [128, D], F32)
        Xb = mk.sb("Xb", [128, D], BF16)
        Xdb = mk.sb("Xdb", [128, D], BF16)
        Btm = mk.sb("Btm", [128, 4, 128], BF16)
        lhall = mk.sb("lhall", [128, 16, 128], F32)
        cbm = mk.sb("cbm", [128, 4, 128], F32)
        Eh = mk.sb("Eh", [128, 4, 128], F32)
        Mall = mk.sb("Mall", [128, 16, 128], BF16)
        yA = mk.sb("yA", [128, D], F32)
        yB = mk.sb("yB", [128, D], F32)
        y5b = mk.sb("y5b", [128, D], BF16)
        y5T = mk.sb("y5T", [128, 8, 128], BF16)
        h1t2 = [mk.sb(f"h1t{i}", [128, D], F32) for i in range(2)]
        prev = mk.sb("prev", [128, D], F32)
        prevb = mk.sb("prevb", [128, D], BF16)
        sm2_ = [mk.sb(f"sm{i}", [128, 12, 16], F32) for i in range(2)]
        ss = mk.sb("ss", [128, 8], F32)
        W0 = mk.ps("W0", [128, D], F32)
        W1 = mk.ps("W1", [128, D], F32)
        S0 = mk.ps("S0", [128, 512], F32)
        CB = mk.ps("CB", [128, 4, 128], F32)
        D0 = mk.ps("D0", [128, 4, 128], F32)
        pTb = mk.ps("pTb", [128, 8, 128], BF16)

        for b_ in (prev, prevb, rawA, rawB):
            POOL.op("memset", [b_[:]], [], ap=b_[:], constant=0.0)

        def v3(b, n=16, m=64):
            return b.v(b.t[:].rearrange("p (h d) -> p h d", h=n))

        def bc3(view2, n=16, m=64):
            return View(view2.buf, view2.ap.unsqueeze(2).to_broadcast([128, n, m]))

        cwA = [View(cw, cw.t[:, 0:8, k:k + 1].to_broadcast([128, 8, 128])) for k in range(4)]
        cwB = [View(cw, cw.t[:, 8:16, k:k + 1].to_broadcast([128, 8, 128])) for k in range(4)]
        cbA = View(cb, cb.t[:, 0:8].unsqueeze(2).to_broadcast([128, 8, 128]))
        cbB = View(cb, cb.t[:, 8:16].unsqueeze(2).to_broadcast([128, 8, 128]))

        def stage1(c):
            xt = xin[c % 2]
            sz, sg, xsf, bcb, sm = sz2[c % 2], sg2_[c % 2], xsf2[c % 2], bcb2[c % 2], sm2_[c % 2]
            make_xT(xt, xb, pTb, xT)
            yield
            for q in (2, 3, 0, 1):
                for j in range(4):
                    cc = q * 4 + j
                    MM(CB[:, j, :], [(wxbc[:, k, cc * 128:(cc + 1) * 128], xT[:, k, :]) for k in range(8)])
                dst = rawA if q < 2 else rawB
                E(ACT, "activation", out=dst[:, (q % 2) * 4:(q % 2) * 4 + 4, 3:131], in_=CB[:], func=AF.Copy)
                if q == 3:
                    conv_half(1)
                if q == 1:
                    conv_half(0)
                yield
            proj1024(W0, xT, wz)
            E(ACT, "activation", out=sz[:], in_=W0[:], func=AF.Silu)
            yield
            proj1024(W1, xT, wg)
            E(ACT, "activation", out=sg[:], in_=W1[:], func=AF.Sigmoid)
            yield
            MM(S0[:, 0:16], [(xT[:, k, :], wdt[:, k, :]) for k in range(8)])
            xx, ax, ee, dt, dA, acs, dd, ds, ea, cd, dtds = [sm[:, i, :] for i in range(11)]
            E(DVE, "tensor_tensor", out=xx, in0=S0[:, 0:16], in1=dtb, op=ALU.add)
            E(DVE, "tensor_scalar", out=ax, in0=xx, scalar1=-1.0, scalar2=None, op0=ALU.mult)
            E(DVE, "tensor_tensor", out=ax, in0=ax, in1=xx, op=ALU.min)
            E(ACT, "activation", out=ee, in_=ax, func=AF.Exp)
            E(ACT, "activation", out=ee, in_=ee, func=AF.Ln, bias=1.0)
            E(DVE, "scalar_tensor_tensor", out=dt, in0=xx, scalar=0.0, in1=ee, op0=ALU.max, op1=ALU.add)
            E(DVE, "tensor_tensor", out=dA, in0=dt, in1=a_bc[:], op=ALU.mult)
            yield
            MM(S0[:, 16:32], [(tri, dA)])
            MM(S0[:, 32:48], [(ones, dA)])
            E(ACT, "activation", out=acs, in_=S0[:, 16:32], func=AF.Copy)
            E(DVE, "tensor_tensor", out=dd, in0=S0[:, 32:48], in1=acs, op=ALU.subtract)
            E(ACT, "activation", out=ds, in_=dd, func=AF.Exp)
            E(ACT, "activation", out=ea, in_=acs, func=AF.Exp)
            E(ACT, "activation", out=cd, in_=S0[:, 32:48], func=AF.Exp)
            E(DVE, "tensor_tensor", out=dtds, in0=dt, in1=ds, op=ALU.mult)

        def conv_half(hb_):
            eng = POOL
            raw, cv, tmpc, cwk, cbk = ((rawA, cvA, ctA, cwA, cbA), (rawB, cvB, ctB, cwB, cbB))[hb_]
            E(eng, "tensor_tensor", out=cv[:], in0=raw[:, :, 0:128], in1=cwk[0], op=ALU.mult)
            E(eng, "tensor_tensor", out=cv[:], in0=cv[:], in1=cbk, op=ALU.add)
            for k in range(1, 4):
                E(eng, "tensor_tensor", out=tmpc[:], in0=raw[:, :, k:k + 128], in1=cwk[k], op=ALU.mult)
                E(eng, "tensor_tensor", out=cv[:], in0=cv[:], in1=tmpc[:], op=ALU.add)
            E(eng, "tensor_copy", out=raw[:, :, 0:3], in_=raw[:, :, 128:131])

        def stage1b_(c):
            pass

        def stage1c_(c):
            xsf, bcb = xsf2[c % 2], bcb2[c % 2]
            E(ACT, "activation", out=bcb[:], in_=cvB[:], func=AF.Silu)
            E(ACT, "activation", out=xsf[:], in_=cvA[:], func=AF.Silu)

        def stage2(c):
            sz, sg, xsf, bcb, sm = sz2[c % 2], sg2_[c % 2], xsf2[c % 2], bcb2[c % 2], sm2_[c % 2]
            xx, ax, ee, dt, dA, acs, dd, ds, ea, cd, dtds = [sm[:, i, :] for i in range(11)]
            for k in range(8):
                TR(W0[:, k * 128:(k + 1) * 128], xsf[:, k, :], identf)
            E(ACT, "activation", out=xs_tm[:], in_=W0[:], func=AF.Copy)
            E(DVE, "tensor_tensor", out=v3(Xb), in0=v3(xs_tm), in1=bc3(dt), op=ALU.mult)
            E(DVE, "tensor_tensor", out=v3(Xdb), in0=v3(xs_tm), in1=bc3(dtds), op=ALU.mult)
            for g in range(4):
                TR(pTb[:, g, :], bcb[:, g, :], idb[:])
            E(DVE, "tensor_copy", out=Btm[:], in_=pTb[:, 0:4, :])
            for g in range(4):
                MM(CB[:, g, :], [(bcb[:, g, :], bcb[:, 4 + g, :])])
            E(DVE, "tensor_tensor", out=cbm[:], in0=CB[:],
              in1=View(cf, tri.ap.unsqueeze(1).to_broadcast([128, 4, 128])), op=ALU.mult)
            E(DVE, "tensor_tensor", out=lhall[:],
              in0=View(cf, ustr.ap.unsqueeze(1).to_broadcast([128, 16, 128])),
              in1=View(dA.buf, dA.ap.unsqueeze(2).to_broadcast([128, 16, 128])), op=ALU.mult)
            yield
            for g in range(4):
                for r in range(4):
                    MM(D0[:, r, :], [(lhall[:, g * 4 + r, :], tri)])
                E(ACT, "activation", out=Eh[:], in_=D0[:], func=AF.Exp)
                E(DVE, "tensor_tensor", out=Mall[:, g * 4:(g + 1) * 4, :], in0=Eh[:],
                  in1=View(cbm, cbm.t[:, g:g + 1, :].to_broadcast([128, 4, 128])), op=ALU.mult)
                yield
            for g in range(4):
                MM(W1[:, g * 256:(g + 1) * 256], [(bcb[:, 4 + g, :], prevb[:, g * 256:(g + 1) * 256])])
            for h in range(16):
                MM(W0[:, h * 64:(h + 1) * 64], [(Mall[:, h, :], Xb[:, h * 64:(h + 1) * 64])])
            E(DVE, "tensor_tensor", out=v3(yA), in0=v3(W1), in1=bc3(ea), op=ALU.mult)
            E(DVE, "tensor_tensor", out=yA[:], in0=W0[:], in1=yA[:], op=ALU.add)
            E(DVE, "tensor_tensor", out=v3(yB), in0=v3(xs_tm), in1=dsk3, op=ALU.mult)
            E(DVE, "tensor_tensor", out=yA[:], in0=yA[:], in1=yB[:], op=ALU.add)
            E(DVE, "tensor_tensor", out=yA[:], in0=yA[:], in1=sz[:], op=ALU.mult)
            for g in range(4):
                E(ACT, "activation", out=yB[:, g * 256:(g + 1) * 256], in_=yA[:, g * 256:(g + 1) * 256],
                  func=AF.Square, accum_out=ss[:, g:g + 1])
            E(ACT, "activation", out=ss[:, 4:8], in_=ss[:, 0:4], func=AF.Sqrt, bias=epsb[:], scale=1.0 / 256)
            E(DVE, "reciprocal", out=ss[:, 4:8], in_=ss[:, 4:8])
            E(DVE, "tensor_tensor", out=v3(yA, 4), in0=v3(yA, 4),
              in1=View(ss, ss.t[:, 4:8].unsqueeze(2).to_broadcast([128, 4, 256])), op=ALU.mult)
            E(DVE, "tensor_tensor", out=y5b[:], in0=yA[:], in1=normw[:], op=ALU.mult)
            yield
            for g in range(4):
                MM(W1[:, g * 256:(g + 1) * 256], [(Btm[:, g, :], Xdb[:, g * 256:(g + 1) * 256])])
            E(DVE, "tensor_tensor", out=v3(prev), in0=v3(prev), in1=bc3(cd), op=ALU.mult)
            E(DVE, "tensor_tensor", out=prev[:], in0=W1[:], in1=prev[:], op=ALU.add)
            E(ACT, "activation", out=prevb[:], in_=prev[:], func=AF.Copy)
            yield
            for k in range(8):
                TR(pTb[:, k, :], y5b[:, k * 128:(k + 1) * 128], idb[:])
            E(DVE, "tensor_copy", out=y5T[:], in_=pTb[:])
            proj1024(W0, y5T, wps)
            h1t = h1t2[c % 2]
            E(DVE, "tensor_tensor", out=h1t[:], in0=W0[:], in1=sg[:], op=ALU.mult)
            SP.dma(H1t[c][:, :], h1t[:])

        SP.dma(xin[0][:], Xcur[0][:, :])
        if NT > 1:
            SP.dma(xin[1][:], Xcur[1][:, :])
        run_all(stage1(0))
        stage1b_(0)
        stage1c_(0)
        for c in range(NT):
            if c + 1 < NT:
                interleave(stage1(c + 1), stage2(c))
                stage1b_(c + 1)
                if c + 2 < NT:
                    SP.dma(xin[c % 2][:], Xcur[c + 2][:, :])
                stage1c_(c + 1)
            else:
                run_all(stage2(c))
        mk.end_phase()
        if stop_after == "A":
            mk.finish()
            return nc

        mk.begin_phase()
        wu = mk.sb("wu", [128, 8, 1024], BF16)
        wv_ = mk.sb("wv", [128, 8, 1024], BF16)
        wgg = mk.sb("wgg", [128, 8, 1024], BF16)
        wpg = mk.sb("wpg", [128, 8, 1024], BF16)
        wout = mk.sb("wout", [128, 8, 1024], BF16)
        wload(POOL, wv_, w_in, win2, OFF_V, OFF_V + 1024)
        wload(POOL, wu, w_in, win2, OFF_U, OFF_U + 1024)
        wload(POOL, wgg, w_in, win2, OFF_G + 1024, OFF_G + 2048)
        wload(POOL, wpg, p_gm, p_gm.t[l], 0, 1024)
        wload(POOL, wout, w_out, w_out.t[l], 0, 1024)
        wsp_f = mk.sb("wsp_f", [128, 8, 128], F32)
        wcT = mk.sb("wcT", [128, 8, 128], BF16)
        SP.dma(wsp_f[:], w_spT[l, :, :, :])
        E(DVE, "tensor_tensor", out=wcT[:], in0=wsp_f[:],
          in1=View(cf, tri.ap.unsqueeze(1).to_broadcast([128, 8, 128])), op=ALU.mult)
        bsp = mk.sb("bsp", [128, 8], F32)
        SP.dma(bsp[:], b_spT[l, :, :])
        gg = mk.sb("gg", [128, D], F32)
        gb = mk.sb("gb", [128, D], F32)
        lg = mk.sb("lg", [128, D], F32)
        lb = mk.sb("lb", [128, D], F32)
        bc_load(gg, gm_ln.t[l, 0:1, :])
        bc_load(gb, gm_ln.t[l, 1:2, :])
        bc_load(lg, ln_gb.t[l, 0, 0:1, :])
        bc_load(lb, ln_gb.t[l, 0, 1:2, :])
        xin = [mk.sb(f"xin{i}", [128, D], F32) for i in range(3)]
        h1in = [mk.sb(f"h1in{i}", [128, D], F32) for i in range(2)]
        st2 = mk.sb("st2", [128, 12], F32)
        mv2 = mk.sb("mv2", [128, 2], F32)
        rs2 = mk.sb("rs2", [128, 1], F32)
        xb = mk.sb("xb", [128, D], BF16)
        xT = mk.sb("xT", [128, 8, 128], BF16)
        gu2 = [mk.sb(f"gu{i}", [128, D], F32) for i in range(2)]
        gv2 = [mk.sb(f"gv{i}", [128, D], F32) for i in range(2)]
        vn2 = [mk.sb(f"vn{i}", [128, D], BF16) for i in range(2)]
        ygm = mk.sb("ygm", [128, D], BF16)
        ygf = mk.sb("ygf", [128, D], F32)
        yT = mk.sb("yT", [128, 8, 128], BF16)
        sg22 = [mk.sb(f"sg2{i}", [128, D], F32) for i in range(2)]
        hf_ = mk.sb("hf", [128, D], F32)
        hb = mk.sb("hb", [128, D], BF16)
        hT = mk.sb("hT", [128, 8, 128], BF16)
        xo = [mk.sb(f"xo{i}", [128, D], F32) for i in range(2)]
        st = mk.sb("st", [128, 12], F32)
        mv = mk.sb("mv", [128, 2], F32)
        rs = mk.sb("rs", [128, 1], F32)
        W0 = mk.ps("W0", [128, D], F32)
        W1 = mk.ps("W1", [128, D], F32)
        W2 = mk.ps("W2", [128, D], F32)
        pTb = mk.ps("pTb", [128, 8, 128], BF16)

        def v3b(b):
            return b.v(b.t[:].rearrange("p (h d) -> p h d", h=8))

        def stage1b(c):
            xt = xin[c % 3]
            gu, vn, sg2 = gu2[c % 2], vn2[c % 2], sg22[c % 2]
            make_xT(xt, xb, pTb, xT)
            yield
            proj1024(W1, xT, wv_)
            E(ACT, "activation", out=gv2[c % 2][:], in_=W1[:], func=AF.Gelu)
            yield
            proj1024(W0, xT, wu)
            E(ACT, "activation", out=gu[:], in_=W0[:], func=AF.Gelu)
            yield
            proj1024(W2, xT, wgg)
            E(ACT, "activation", out=sg2[:], in_=W2[:], func=AF.Sigmoid)

        def stage1bb(c):
            layer_norm(gv2[c % 2], vn2[c % 2][:], gg[:], gb[:], st, mv, rs)

        def stage2b(c):
            xt = xin[c % 3]
            h1 = h1in[c % 2]
            gu, vn, sg2 = gu2[c % 2], vn2[c % 2], sg22[c % 2]
            for g in range(8):
                MM(W0[:, g * 128:(g + 1) * 128], [(wcT[:, g, :], vn[:, g * 128:(g + 1) * 128])])
            E(DVE, "tensor_tensor", out=v3b(ygf), in0=v3b(W0),
              in1=View(bsp, bsp.t[:, :].unsqueeze(2).to_broadcast([128, 8, 128])), op=ALU.add)
            E(DVE, "tensor_tensor", out=ygm[:], in0=ygf[:], in1=gu[:], op=ALU.mult)
            yield
            for k in range(8):
                TR(pTb[:, k, :], ygm[:, k * 128:(k + 1) * 128], idb[:])
            E(DVE, "tensor_copy", out=yT[:], in_=pTb[:])
            proj1024(W1, yT, wpg)
            E(DVE, "tensor_tensor", out=hf_[:], in0=W1[:], in1=sg2[:], op=ALU.mult)
            E(DVE, "tensor_tensor", out=hb[:], in0=hf_[:], in1=h1[:], op=ALU.add)
            yield
            for k in range(8):
                TR(pTb[:, k, :], hb[:, k * 128:(k + 1) * 128], idb[:])
            E(DVE, "tensor_copy", out=hT[:], in_=pTb[:])
            yield
            proj1024(W2, hT, wout)
            xo_ = xo[c % 2]
            E(DVE, "scalar_tensor_tensor", out=xo_[:], in0=xt[:], scalar=ALPHA, in1=W2[:], op0=ALU.mult, op1=ALU.add)
            layer_norm(xo_, xo_[:], lg[:], lb[:], st2, mv2, rs2)
            SP.dma(X1t[c][:, :], xo_[:])

        for c0 in range(min(2, NT)):
            SP.dma(xin[c0][:], Xcur[c0][:, :])
        SP.dma(h1in[0][:], H1t[0][:, :])
        run_all(stage1b(0))
        stage1bb(0)
        for c in range(NT):
            if c + 1 < NT:
                SP.dma(h1in[(c + 1) % 2][:], H1t[c + 1][:, :])
                if c + 2 < NT:
                    SP.dma(xin[(c + 2) % 3][:], Xcur[c + 2][:, :])
                interleave(stage1b(c + 1), stage2b(c))
                stage1bb(c + 1)
            else:
                run_all(stage2b(c))
        mk.end_phase()
        if stop_after == "B":
            mk.finish()
            return nc

        mk.begin_phase()
        wq_ = mk.sb("wq", [128, 8, 1024], BF16)
        wo_ = mk.sb("wo", [128, 8, 1024], BF16)
        wk_ = mk.sb("wk", [128, 8, 1024], BF16)
        wvv = mk.sb("wvv", [128, 8, 1024], BF16)
        wload(POOL, wk_, wk, wk.t[l], 0, 1024)
        wload(POOL, wvv, wv, wv.t[l], 0, 1024)
        wload(POOL, wq_, wq, wq.t[l], 0, 1024)
        wload(POOL, wo_, wo, wo.t[l], 0, 1024)
        kT = mk.sb("kT", [128, 8, MEM], BF16)
        vtm = mk.sb("vtm", [128, 2, D], BF16)
        lg = mk.sb("lg", [128, D], F32)
        lb = mk.sb("lb", [128, D], F32)
        bc_load(lg, ln_gb.t[l, 1, 0:1, :])
        bc_load(lb, ln_gb.t[l, 1, 1:2, :])
        xin = [mk.sb(f"xin{i}", [128, D], F32) for i in range(3)]
        xb = mk.sb("xb", [128, D], BF16)
        xT = mk.sb("xT", [128, 8, 128], BF16)
        qT = mk.sb("qT", [128, 8, 128], BF16)
        Pf = mk.sb("Pf", [128, 4, MEM], F32)
        Pn2 = [mk.sb(f"Pn{i}", [128, 4, MEM], BF16) for i in range(2)]
        PT = mk.sb("PT", [128, 8, 128], BF16)
        oT = mk.sb("oT", [128, 8, 128], BF16)
        mx = mk.sb("mx", [128, 12], F32)
        xo = [mk.sb(f"xo{i}", [128, D], F32) for i in range(2)]
        st = mk.sb("st", [128, 12], F32)
        mv = mk.sb("mv", [128, 2], F32)
        rs = mk.sb("rs", [128, 1], F32)
        W0 = mk.ps("W0", [128, D], F32)
        W1 = mk.ps("W1", [128, D], F32)
        W2 = mk.ps("W2", [128, D], F32)
        pTb = mk.ps("pTb", [128, 8, 128], BF16)
        for cc in range(8):
            MM(W0[:, (cc % 4) * 256:(cc % 4) * 256 + 256],
               [(wk_[:, k, cc * 128:(cc + 1) * 128], memT[:, k, :]) for k in range(8)])
            if cc % 4 == 3:
                E(ACT, "activation", out=kT[:, cc - 3:cc + 1, :],
                  in_=W0.v(W0.t[:].rearrange("p (a m) -> p a m", a=4)), func=AF.Copy)
        for a in range(2):
            proj1024(W1, memT.v(memT.t[:, :, a * 128:(a + 1) * 128]), wvv)
            E(ACT, "activation", out=vtm[:, a, :], in_=W1[:], func=AF.Copy)

        def stage1c(c):
            xt = xin[c % 3]
            Pn = Pn2[c % 2]
            make_xT(xt, xb, pTb, xT)
            for qc in range(8):
                MM(W0[:, qc * 128:(qc + 1) * 128],
                   [(wq_[:, k, qc * 128:(qc + 1) * 128], xT[:, k, :]) for k in range(8)])
            E(ACT, "activation", out=qT[:], in_=W0.v(W0.t[:].rearrange("p (a m) -> p a m", a=8)),
              func=AF.Copy, scale=1.0 / 16.0)
            yield
            for h in range(4):
                MM(W1[:, h * 256:(h + 1) * 256],
                   [(qT[:, 2 * h + dc, :], kT[:, 2 * h + dc, :]) for dc in range(2)])

        def stage1cb(c):
            Pn = Pn2[c % 2]
            W1v = W1.v(W1.t[:].rearrange("p (h m) -> p h m", h=4))
            E(DVE, "tensor_reduce", out=mx[:, 0:4], in_=W1v, axis=AX.X, op=ALU.max)
            E(DVE, "tensor_scalar", out=mx[:, 4:8], in0=mx[:, 0:4], scalar1=-1.0, scalar2=None, op0=ALU.mult)
            for h in range(4):
                E(ACT, "activation", out=Pf[:, h, :], in_=W1[:, h * 256:(h + 1) * 256], func=AF.Exp,
                  bias=mx[:, 4 + h:5 + h], accum_out=mx[:, 8 + h:9 + h])
            E(DVE, "reciprocal", out=mx[:, 0:4], in_=mx[:, 8:12])
            E(DVE, "tensor_tensor", out=Pn[:], in0=Pf[:],
              in1=View(mx, mx.t[:, 0:4].unsqueeze(2).to_broadcast([128, 4, MEM])), op=ALU.mult)

        def stage2c(c):
            xt = xin[c % 3]
            Pn = Pn2[c % 2]
            for h in range(4):
                for mc in range(2):
                    TR(pTb[:, 2 * h + mc, :], Pn[:, h, mc * 128:(mc + 1) * 128], idb[:])
            E(DVE, "tensor_copy", out=PT[:], in_=pTb[:])
            yield
            for oc in range(8):
                h = oc // 2
                MM(W0[:, oc * 128:(oc + 1) * 128],
                   [(vtm[:, mc, oc * 128:(oc + 1) * 128], PT[:, 2 * h + mc, :]) for mc in range(2)])
            E(ACT, "activation", out=oT[:], in_=W0.v(W0.t[:].rearrange("p (a m) -> p a m", a=8)), func=AF.Copy)
            yield
            proj1024(W2, oT, wo_)
            xo_ = xo[c % 2]
            E(DVE, "scalar_tensor_tensor", out=xo_[:], in0=xt[:], scalar=ALPHA, in1=W2[:], op0=ALU.mult, op1=ALU.add)
            layer_norm(xo_, xo_[:], lg[:], lb[:], st, mv, rs)
            SP.dma(X2t[c][:, :], xo_[:])

        for c0 in range(min(2, NT)):
            SP.dma(xin[c0][:], X1t[c0][:, :])
        run_all(stage1c(0))
        stage1cb(0)
        for c in range(NT):
            if c + 1 < NT:
                if c + 2 < NT:
                    SP.dma(xin[(c + 2) % 3][:], X1t[c + 2][:, :])
                interleave(stage1c(c + 1), stage2c(c))
                stage1cb(c + 1)
            else:
                run_all(stage2c(c))
        mk.end_phase()
        if stop_after == "C":
            mk.finish()
            return nc

        TB = min(S, 1024)
        NB = S // TB
        TT = TB // 128
        SUBN = min(TB, 512)
        NSUB = TB // SUBN
        _NEXP = int(_os.environ.get('KDBG_NEXP', NE))
        mk.begin_phase()
        lg = mk.sb("lg", [128, D], F32)
        lb = mk.sb("lb", [128, D], F32)
        bc_load(lg, ln_gb.t[l, 2, 0:1, :])
        bc_load(lb, ln_gb.t[l, 2, 1:2, :])
        wr = mk.sb("wr", [128, 8, NE], F32)
        SP.dma(wr[:], dsrc(w_router, w_router.t[l].rearrange("(k p) n -> p k n", p=128)))
        br = mk.sb("br", [128, NE], F32)
        bc_load(br, b_router.t[l:l + 1, :])
        bgu = mk.sb("bgu", [128, NE, 16], F32)
        SP.dma(bgu[:], b_guT[l, :, :, :])
        E(DVE, "tensor_scalar", out=bgu[:, :, 0:8], in0=bgu[:, :, 0:8], scalar1=-1.0, scalar2=7.0, op0=ALU.mult, op1=ALU.add)
        E(DVE, "tensor_scalar", out=bgu[:, :, 8:16], in0=bgu[:, :, 8:16], scalar1=7.0, scalar2=None, op0=ALU.add)
        bdn = mk.sb("bdn", [128, D], F32)
        POOL.op("memset", [bdn[:]], [], ap=bdn[:], constant=0.0)
        SP.dma(bdn[0:NE, :], b_down[l, :, :])
        gpad_ = [mk.sb(f"gpad{i}", [128, 128], F32) for i in range(2)]
        for gp_ in gpad_:
            POOL.op("memset", [gp_[:]], [], ap=gp_[:], constant=0.0)
        wgu_b = [mk.sb(f"wgu{i}", [128, 8, 2 * D], BF16) for i in range(2)]
        wdn_b = [mk.sb(f"wdn{i}", [128, 8, D], BF16) for i in range(2)]
        acc = [mk.sb(f"acc{i}", [128, D], F32) for i in range(TT)]
        xTm = mk.sb("xTm", [128, 8, TB], BF16)
        lgt_ = [mk.sb(f"lgt{i}", [128, NE], F32) for i in range(2)]
        m8_ = [mk.sb(f"m8{i}", [128, 8], F32) for i in range(2)]
        msk_ = [mk.sb(f"msk{i}", [128, NE], F32) for i in range(2)]
        ex_ = [mk.sb(f"ex{i}", [128, NE], F32) for i in range(2)]
        sm2_ = [mk.sb(f"sm2{i}", [128, 4], F32) for i in range(2)]
        gates = mk.sb("gates", [128, TT, NE], F32)
        gT_ = [mk.sb(f"gT{i}", [128, 128], F32) for i in range(2)]
        actT = [mk.sb(f"actT{i}", [128, 8, SUBN], BF16) for i in range(2)]
        NTMP = 3
        tmps = [[mk.sb(f"tm{i}_{q}", [128, SUBN], F32) for q in range(3)] for i in range(NTMP)]
        st = mk.sb("st", [128, 12], F32)
        mv = mk.sb("mv", [128, 2], F32)
        rs = mk.sb("rs", [128, 1], F32)
        HGg = [mk.ps(f"HGg{i}", [128, 512], F32) for i in range(3)]
        HGu = [mk.ps(f"HGu{i}", [128, 512], F32) for i in range(3)]
        YPh = [mk.ps(f"YPh{i}", [128, 512], F32) for i in range(2)]

        def load_expert(e, slot):
            src = w_gu.t[l, e].rearrange("(k p) n -> p k n", p=128)
            for q in range(4):
                POOL.dma(wgu_b[slot][:, 2 * q:2 * q + 2, :], dsrc(w_gu, src[:, 2 * q:2 * q + 2, :]))
            src2 = w_down.t[l, e].rearrange("(k p) n -> p k n", p=128)
            for q in range(2):
                POOL.dma(wdn_b[slot][:, 4 * q:4 * q + 4, :], dsrc(w_down, src2[:, 4 * q:4 * q + 4, :]))

        units = [(e, sb_) for e in range(_NEXP) for sb_ in range(NSUB)]
        pstate = {"pi": 0}

        def emit_hgu(u, blk):
            e, sb_ = units[u]
            wg_ = wgu_b[e % 2]
            t0 = sb_ * SUBN
            aT = actT[u % 2]
            for j in range(8):
                pi = pstate["pi"]
                pstate["pi"] += 1
                gb, ub = HGg[pi % 3], HGu[pi % 3]
                rg, b_, r_ = tmps[pi % NTMP]
                MM(gb[:, 0:SUBN], [(wg_[:, k, j * 128:(j + 1) * 128], xTm[:, k, t0:t0 + SUBN]) for k in range(8)])
                MM(ub[:, 0:SUBN], [(wg_[:, k, D + j * 128:D + (j + 1) * 128], xTm[:, k, t0:t0 + SUBN]) for k in range(8)])
                E(ACT, "activation", out=rg[:], in_=gb[:, 0:SUBN], func=AF.Relu, scale=-1.0, bias=bgu[:, e, j:j + 1])
                E(ACT, "activation", out=r_[:], in_=ub[:, 0:SUBN], func=AF.Relu, bias=bgu[:, e, 8 + j:9 + j])
                E(ACT, "activation", out=b_[:], in_=rg[:], func=AF.Sigmoid, scale=-SWA, bias=c7a[:])
                E(ACT, "activation", out=r_[:], in_=r_[:], func=AF.Relu, scale=-1.0, bias=c14[:])
                E(DVE, "scalar_tensor_tensor", out=rg[:], in0=rg[:], scalar=7.0, in1=b_[:], op0=ALU.subtract, op1=ALU.mult)
                E(DVE, "scalar_tensor_tensor", out=aT[:, j, :], in0=r_[:], scalar=8.0, in1=rg[:], op0=ALU.subtract, op1=ALU.mult)

        def emit_down(u, blk):
            e, sb_ = units[u]
            wd_ = wdn_b[e % 2]
            aT = actT[u % 2]
            for t4 in range(SUBN // 128):
                tt = sb_ * (SUBN // 128) + t4
                for hf in range(2):
                    yp = YPh[hf]
                    MM(yp[:, :], [(aT[:, j, t4 * 128:(t4 + 1) * 128], wd_[:, j, hf * 512:(hf + 1) * 512]) for j in range(8)])
                    E(DVE, "scalar_tensor_tensor", out=acc[tt][:, hf * 512:(hf + 1) * 512], in0=yp[:, :],
                      scalar=gates[:, tt, e:e + 1], in1=acc[tt][:, hf * 512:(hf + 1) * 512], op0=ALU.mult, op1=ALU.add)

        SWA = 1.702
        c7a = mk.sb("c7a", [128, 1], F32)
        c14 = mk.sb("c14", [128, 1], F32)
        POOL.op("memset", [c7a[:]], [], ap=c7a[:], constant=7.0 * SWA)
        POOL.op("memset", [c14[:]], [], ap=c14[:], constant=14.0)

        for blk in range(NB):
            if _NEXP > 0:
                load_expert(0, 0)
            if _NEXP > 1:
                load_expert(1, 1)
            for tt in range(TT):
                SP.dma(acc[tt][:], X2t[blk * TT + tt][:, :])
            for tt in range(TT):
                xt = acc[tt]
                p2 = tt % 2
                PA, PB = HGg[tt % 3], HGu[tt % 3]
                xa, xb_ = tmps[tt % 3][0], tmps[tt % 3][1]
                xa3 = xa.v(xa.t[:].rearrange("p (a m) -> p a m", a=4))
                xb3 = xb_.v(xb_.t[:].rearrange("p (a m) -> p a m", a=4))
                lgt, m8, msk, ex, sm2, gT, gpad = lgt_[p2], m8_[p2], msk_[p2], ex_[p2], sm2_[p2], gT_[p2], gpad_[p2]
                for k in range(8):
                    dstp = PA if k < 4 else PB
                    TR(dstp[:, (k % 4) * 128:(k % 4 + 1) * 128], xt[:, k * 128:(k + 1) * 128], identf)
                E(ACT, "activation", out=xa3, in_=PA.v(PA.t[:].rearrange("p (a m) -> p a m", a=4)), func=AF.Copy)
                E(ACT, "activation", out=xb3, in_=PB.v(PB.t[:].rearrange("p (a m) -> p a m", a=4)), func=AF.Copy)
                E(DVE, "tensor_copy", out=xTm[:, 0:4, tt * 128:(tt + 1) * 128], in_=xa3)
                E(DVE, "tensor_copy", out=xTm[:, 4:8, tt * 128:(tt + 1) * 128], in_=xb3)
                RP = YPh[p2]
                MM(RP[:, 0:NE], [((xa3 if k < 4 else xb3)[:, k % 4, :], wr[:, k, :]) for k in range(8)])
                E(DVE, "tensor_tensor", out=lgt[:], in0=RP[:, 0:NE], in1=br[:], op=ALU.add)
                E(DVE, "max", out=m8[:], in_=lgt[:])
                E(DVE, "tensor_scalar", out=msk[:], in0=lgt[:], scalar1=m8[:, 3:4], scalar2=None, op0=ALU.is_ge)
                E(DVE, "tensor_scalar", out=sm2[:, 0:1], in0=m8[:, 0:1], scalar1=-1.0, scalar2=None, op0=ALU.mult)
                E(ACT, "activation", out=ex[:], in_=lgt[:], func=AF.Exp, bias=sm2[:, 0:1])
                E(DVE, "tensor_tensor", out=ex[:], in0=ex[:], in1=msk[:], op=ALU.mult)
                E(DVE, "reduce_sum", out=sm2[:, 1:2], in_=ex[:], axis=AX.X)
                E(DVE, "reciprocal", out=sm2[:, 2:3], in_=sm2[:, 1:2])
                E(DVE, "tensor_scalar", out=gates[:, tt, :], in0=ex[:], scalar1=sm2[:, 2:3], scalar2=None, op0=ALU.mult)
                E(DVE, "tensor_copy", out=gpad[:, 0:NE], in_=gates[:, tt, :])
                TR(RP[:, 128:256], gpad[:], identf)
                E(ACT, "activation", out=gT[:], in_=RP[:, 128:256], func=AF.Copy)
                MM(PA[:, :], [(gT[:], bdn[:, 0:512])])
                MM(PB[:, :], [(gT[:], bdn[:, 512:1024])])
                E(DVE, "scalar_tensor_tensor", out=xt[:, 0:512], in0=xt[:, 0:512], scalar=ALPHA, in1=PA[:, :],
                  op0=ALU.mult, op1=ALU.add)
                E(DVE, "scalar_tensor_tensor", out=xt[:, 512:1024], in0=xt[:, 512:1024], scalar=ALPHA, in1=PB[:, :],
                  op0=ALU.mult, op1=ALU.add)
            NU = len(units)
            if NU > 0:
                emit_hgu(0, blk)
            for u in range(NU):
                if u + 1 < NU:
                    emit_hgu(u + 1, blk)
                emit_down(u, blk)
                e, sb_ = units[u]
                if sb_ == NSUB - 1 and e + 2 < _NEXP:
                    load_expert(e + 2, e % 2)
            for tt in range(TT):
                layer_norm_view(acc[tt][:], acc[tt], lg, lb, st, mv, rs)
                SP.dma(X3t[blk * TT + tt][:, :], acc[tt][:])
        mk.end_phase()

        Xcur = X3t

    mk.finish()
    return nc


def layer_norm_view(at, o_, lg, lb, st, mv, rs):
    mk = o_.mk
    DVE, POOL, ACT = mk.DVE, mk.POOL, mk.ACT
    a0 = View(at.buf, at.ap[:, 0:512])
    a1 = View(at.buf, at.ap[:, 512:1024])
    E(DVE, "bn_stats", out=st[:, 0:6], in_=a0)
    E(DVE, "bn_stats", out=st[:, 6:12], in_=a1)
    E(DVE, "bn_aggr", out=mv[:], in_=st[:])
    E(ACT, "activation", out=rs[:], in_=mv[:, 1:2], func=AF.Sqrt, bias=o_.mk.epsb[:], scale=1.0)
    E(DVE, "reciprocal", out=rs[:], in_=rs[:])
    E(DVE, "tensor_scalar", out=o_[:], in0=at, scalar1=mv[:, 0:1], scalar2=rs[:], op0=ALU.subtract, op1=ALU.mult)
    E(DVE, "tensor_tensor", out=o_[:], in0=o_[:], in1=lg[:], op=ALU.mult)
    E(DVE, "tensor_tensor", out=o_[:], in0=o_[:], in1=lb[:], op=ALU.add)


def host_inputs(inp, b, S):
    f = np.float32
    L = inp["w_in"].shape[0]
    tri = np.triu(np.ones((128, 128), f))
    ustr = np.tril(np.ones((128, 128), f), -1)
    cf = np.stack([np.eye(128, dtype=f), tri, ustr, np.ones((128, 128), f)], axis=1)
    m = {
        "x": np.ascontiguousarray(inp["x"][b, :S]),
        "mem": np.ascontiguousarray(inp["mem"][b]),
        "ln0": np.stack([inp["ln0_g"], inp["ln0_b"]]).astype(f),
        "w_in": inp["w_in"],
        "conv_w": np.ascontiguousarray(inp["conv_w"].reshape(L, 4, 16, 128).transpose(0, 3, 2, 1)),
        "conv_b": np.ascontiguousarray(inp["conv_b"].reshape(L, 16, 128).transpose(0, 2, 1)),
        "vec16": np.ascontiguousarray(np.stack([inp["dt_bias"], inp["a_log"], inp["d_skip"]], axis=1)),
        "ssd_norm_w": inp["ssd_norm_w"],
        "gm_ln": np.ascontiguousarray(np.stack([inp["gm_ln_g"], inp["gm_ln_b"]], axis=1)),
        "w_spT": np.ascontiguousarray(inp["w_sp"].transpose(0, 3, 1, 2)),
        "b_spT": np.ascontiguousarray(inp["b_sp"].transpose(0, 2, 1)),
        "p_ssd": inp["p_ssd"], "p_gm": inp["p_gm"], "w_out": inp["w_out"],
        "wq": inp["wq"], "wk": inp["wk"], "wv": inp["wv"], "wo": inp["wo"],
        "w_router": inp["w_router"], "b_router": inp["b_router"],
        "w_gu": inp["w_gu"],
        "b_guT": np.ascontiguousarray(inp["b_gu"].reshape(L, NE, 16, 128).transpose(0, 3, 1, 2)),
        "w_down": inp["w_down"], "b_down": inp["b_down"],
        "ln_gb": np.ascontiguousarray(np.stack([inp["ln_g"], inp["ln_b"]], axis=2)),
        "cf32": np.ascontiguousarray(cf),
        "cidb": np.eye(128, dtype=f).astype(ml_dtypes.bfloat16),
    }
    return {k: np.ascontiguousarray(np.asarray(v)) for k, v in m.items()}


_NC_CACHE = {}


def kernel(**inputs):
    inp = {k: np.asarray(v) for k, v in inputs.items()}
    B, S, _ = inp["x"].shape
    L = inp["w_in"].shape[0]
    key = (S, L)
    if key not in _NC_CACHE:
        _NC_CACHE[key] = build(S, L)
    nc = _NC_CACHE[key]
    in_maps = [host_inputs(inp, b, S) for b in range(B)]
    res = run_bass_kernel_spmd(nc, in_maps, core_ids=list(range(B)))
    out = np.stack([np.asarray(r["y_out"]) for r in res.results], axis=0)
    return out.astype(np.float32)
```
